# Optimizing a Trainium2 kernel written in Bass

```python
import math
import jax, jax.numpy as jnp
from jax import lax
import numpy as np

D_MODEL = 4096
BATCH = 2
SEQ = 4096
DEPTH = 1

CHUNK = 64

SSD_EXPAND = 2
D_INNER = SSD_EXPAND * D_MODEL
SSD_HEAD_DIM = 64
SSD_HEADS = D_INNER // SSD_HEAD_DIM
SSD_STATE = 128
SSD_GROUPS = 8
SSD_HEADS_PER_GROUP = SSD_HEADS // SSD_GROUPS
SSD_CONV = 4
SSD_CONV_DIM = D_INNER + 2 * SSD_GROUPS * SSD_STATE

FOX_HEAD_DIM = 128
FOX_HEADS = D_MODEL // FOX_HEAD_DIM
D_ATT = FOX_HEADS * FOX_HEAD_DIM
Q_BLOCK = 128

D_FF = ((8 * D_MODEL // 3 + 255) // 256) * 256
FFN_CONV = 3

ALPHA = (2.0 * DEPTH) ** 0.25
BETA = (8.0 * DEPTH) ** -0.25
LN_EPS = 1e-5
RMS_EPS = 1e-5

IN_SIZES = (D_INNER, SSD_CONV_DIM, SSD_HEADS, D_ATT, D_ATT, D_ATT, FOX_HEADS, D_MODEL, D_MODEL)
D_IN_PROJ = sum(IN_SIZES)
IN_SPLITS = tuple(int(s) for s in np.cumsum(IN_SIZES)[:-1])

kernel_name = "hybrid_ssd_fox_convffn_deepnorm"


def _layer_norm(x, g, b):
    xf = x.astype(jnp.float32)
    mu = jnp.mean(xf, axis=-1, keepdims=True)
    xc = xf - mu
    var = jnp.mean(xc * xc, axis=-1, keepdims=True)
    out = xc * lax.rsqrt(var + LN_EPS) * g.astype(jnp.float32) + b.astype(jnp.float32)
    return out.astype(x.dtype)


def _causal_dwconv(x, w, b):
    k_width = w.shape[0]
    length = x.shape[1]
    xp = jnp.pad(x, ((0, 0), (k_width - 1, 0), (0, 0)))
    out = b
    for k in range(k_width):
        out = out + w[k] * xp[:, k:k + length]
    return out


def _ssd_branch(z, xbc, dt_raw, conv_w, conv_b, dt_bias, a_log, d_skip, norm_w):
    bsz, length, _ = z.shape
    nc = length // CHUNK
    G, R, P, N = SSD_GROUPS, SSD_HEADS_PER_GROUP, SSD_HEAD_DIM, SSD_STATE
    f32 = jnp.float32
    xbc = jax.nn.silu(_causal_dwconv(xbc, conv_w, conv_b))
    xs, bm, cm = jnp.split(xbc, [D_INNER, D_INNER + G * N], axis=-1)
    xs = xs.astype(f32).reshape(bsz, length, G, R, P)
    bm = bm.astype(f32).reshape(bsz, nc, CHUNK, G, N)
    cm = cm.astype(f32).reshape(bsz, nc, CHUNK, G, N)
    dt = jax.nn.softplus(dt_raw.astype(f32) + dt_bias.astype(f32)).reshape(bsz, length, G, R)
    a = -jnp.exp(a_log.astype(f32)).reshape(G, R)
    da = (dt * a).reshape(bsz, nc, CHUNK, G, R)
    xdt = (xs * dt[..., None]).reshape(bsz, nc, CHUNK, G, R, P)
    acs = jnp.cumsum(da, axis=2)
    seg = acs[:, :, :, None] - acs[:, :, None, :]
    tri = jnp.tril(jnp.ones((CHUNK, CHUNK), dtype=bool))[None, None, :, :, None, None]
    lmat = jnp.exp(jnp.where(tri, seg, -jnp.inf))
    cb = jnp.einsum("bclgn,bcsgn->bclsg", cm, bm)
    y_diag = jnp.einsum("bclsg,bclsgr,bcsgrp->bclgrp", cb, lmat, xdt)
    decay_to_end = jnp.exp(acs[:, :, -1:] - acs)
    states = jnp.einsum("bcsgn,bcsgr,bcsgrp->bcgrpn", bm, decay_to_end, xdt)
    chunk_decay = jnp.exp(acs[:, :, -1])

    def step(h, inp):
        s_c, d_c = inp
        return d_c[..., None, None] * h + s_c, h

    h0 = jnp.zeros((bsz, G, R, P, N), f32)
    _, prev = lax.scan(step, h0, (jnp.moveaxis(states, 1, 0), jnp.moveaxis(chunk_decay, 1, 0)))
    prev = jnp.moveaxis(prev, 0, 1)
    y_off = jnp.einsum("bclgn,bcgrpn,bclgr->bclgrp", cm, prev, jnp.exp(acs))
    y = (y_diag + y_off).reshape(bsz, length, G, R, P) + d_skip.astype(f32).reshape(G, R)[..., None] * xs
    y = y.reshape(bsz, length, D_INNER) * jax.nn.silu(z.astype(f32))
    yg = y.reshape(bsz, length, G, D_INNER // G)
    yg = yg * lax.rsqrt(jnp.mean(yg * yg, axis=-1, keepdims=True) + RMS_EPS)
    return (yg.reshape(bsz, length, D_INNER) * norm_w.astype(f32)).astype(z.dtype)


def _fox_branch(q, k, v, f_logit):
    bsz, length, _ = q.shape
    H, Dh = FOX_HEADS, FOX_HEAD_DIM
    scale = 1.0 / math.sqrt(Dh)
    q = q.reshape(bsz, length, H, Dh).transpose(0, 2, 1, 3)
    k = k.reshape(bsz, length, H, Dh).transpose(0, 2, 1, 3)
    v = v.reshape(bsz, length, H, Dh).transpose(0, 2, 1, 3)
    logf = jax.nn.log_sigmoid(f_logit.astype(jnp.float32))
    fcum = jnp.cumsum(logf, axis=1).transpose(0, 2, 1)
    outs = []
    for i in range(length // Q_BLOCK):
        lo, hi = i * Q_BLOCK, (i + 1) * Q_BLOCK
        s = jnp.einsum("bhqd,bhkd->bhqk", q[:, :, lo:hi], k[:, :, :hi],
                       preferred_element_type=jnp.float32) * scale
        s = s + fcum[:, :, lo:hi, None] - fcum[:, :, None, :hi]
        mask = (lo + jnp.arange(Q_BLOCK))[:, None] >= jnp.arange(hi)[None, :]
        p = jax.nn.softmax(jnp.where(mask, s, -jnp.inf), axis=-1)
        outs.append(jnp.einsum("bhqk,bhkd->bhqd", p.astype(v.dtype), v[:, :, :hi]))
    o = jnp.concatenate(outs, axis=2)
    return o.transpose(0, 2, 1, 3).reshape(bsz, length, D_ATT)


def setup_inputs(seed: int = 0) -> dict:
    key = jax.random.key(seed)
    ks = jax.random.split(key, 24)
    f32 = jnp.float32

    def nrm(k, shape, scale):
        return scale * jax.random.normal(k, shape, f32)

    dt0 = jnp.exp(jax.random.uniform(ks[4], (DEPTH, SSD_HEADS), f32,
                                     minval=math.log(1e-3), maxval=math.log(1e-1)))
    dt_bias = dt0 + jnp.log(-jnp.expm1(-dt0))
    a_log = jnp.log(jax.random.uniform(ks[5], (DEPTH, SSD_HEADS), f32, minval=1.0, maxval=16.0))
    return {
        "x": nrm(ks[0], (BATCH, SEQ, D_MODEL), 1.0),
        "w_in": nrm(ks[1], (DEPTH, D_MODEL, D_IN_PROJ), D_MODEL ** -0.5),
        "ssd_conv_w": nrm(ks[2], (DEPTH, SSD_CONV, SSD_CONV_DIM), SSD_CONV ** -0.5),
        "ssd_conv_b": nrm(ks[3], (DEPTH, SSD_CONV_DIM), 0.01),
        "ssd_dt_bias": dt_bias,
        "ssd_a_log": a_log,
        "ssd_d": 1.0 + nrm(ks[6], (DEPTH, SSD_HEADS), 0.01),
        "ssd_norm_w": 1.0 + nrm(ks[7], (DEPTH, D_INNER), 0.01),
        "fox_f_bias": 3.0 + nrm(ks[8], (DEPTH, FOX_HEADS), 0.5),
        "gate_bias": nrm(ks[9], (DEPTH, 2, D_MODEL), 0.01),
        "w_proj_ssd": nrm(ks[10], (DEPTH, D_INNER, D_MODEL), D_INNER ** -0.5),
        "w_proj_att": nrm(ks[11], (DEPTH, D_ATT, D_MODEL), D_ATT ** -0.5),
        "w_out": nrm(ks[12], (DEPTH, D_MODEL, D_MODEL), BETA * D_MODEL ** -0.5),
        "ln1_g": 1.0 + nrm(ks[13], (DEPTH, D_MODEL), 0.01),
        "ln1_b": nrm(ks[14], (DEPTH, D_MODEL), 0.01),
        "w_up": nrm(ks[15], (DEPTH, D_MODEL, 2 * D_FF), D_MODEL ** -0.5),
        "ffn_conv_w": nrm(ks[16], (DEPTH, FFN_CONV, 2 * D_FF), FFN_CONV ** -0.5),
        "ffn_conv_b": nrm(ks[17], (DEPTH, 2 * D_FF), 0.01),
        "w_down": nrm(ks[18], (DEPTH, D_FF, D_MODEL), BETA * D_FF ** -0.5),
        "ln2_g": 1.0 + nrm(ks[19], (DEPTH, D_MODEL), 0.01),
        "ln2_b": nrm(ks[20], (DEPTH, D_MODEL), 0.01),
    }


def reference(x, w_in, ssd_conv_w, ssd_conv_b, ssd_dt_bias, ssd_a_log, ssd_d, ssd_norm_w,
              fox_f_bias, gate_bias, w_proj_ssd, w_proj_att, w_out, ln1_g, ln1_b,
              w_up, ffn_conv_w, ffn_conv_b, w_down, ln2_g, ln2_b):
    h = x
    for layer in range(DEPTH):
        proj = jnp.einsum("bld,de->ble", h, w_in[layer])
        z, xbc, dt_raw, q, k, v, f_logit, g_ssd, g_att = jnp.split(proj, IN_SPLITS, axis=-1)
        y_ssd = _ssd_branch(z, xbc, dt_raw, ssd_conv_w[layer], ssd_conv_b[layer],
                            ssd_dt_bias[layer], ssd_a_log[layer], ssd_d[layer], ssd_norm_w[layer])
        y_att = _fox_branch(q, k, v, f_logit + fox_f_bias[layer])
        merged = (jax.nn.sigmoid(g_ssd + gate_bias[layer, 0]) * jnp.einsum("ble,ed->bld", y_ssd, w_proj_ssd[layer])
                  + jax.nn.sigmoid(g_att + gate_bias[layer, 1]) * jnp.einsum("ble,ed->bld", y_att, w_proj_att[layer]))
        mix = jnp.einsum("bld,de->ble", merged, w_out[layer])
        h = _layer_norm(ALPHA * h + mix, ln1_g[layer], ln1_b[layer])
        u = _causal_dwconv(jnp.einsum("bld,df->blf", h, w_up[layer]), ffn_conv_w[layer], ffn_conv_b[layer])
        val, gate = jnp.split(u, 2, axis=-1)
        f = jnp.einsum("blf,fd->bld", jax.nn.silu(gate) * val, w_down[layer])
        h = _layer_norm(ALPHA * h + f, ln2_g[layer], ln2_b[layer])
    return h
```

```python
import contextlib
import numpy as np
import ml_dtypes
import concourse.bass as bass
import concourse.mybir as mybir
from concourse.bass_utils import run_bass_kernel_spmd

F32 = mybir.dt.float32
BF16 = mybir.dt.bfloat16
I32 = mybir.dt.int32
AF = mybir.ActivationFunctionType
ALU = mybir.AluOpType
AX = mybir.AxisListType

L = 4096
NB = 2
NBLKT = L // 128
ALPHA = 2.0 ** 0.25
LN_EPS = 1e-5
RMS_EPS = 1e-5


class Dims:
    def __init__(self, D):
        self.D = D
        self.KC = D // 128
        self.DI = 2 * D
        self.SH = self.DI // 64
        self.R = self.SH // 8
        self.GW = self.R * 64
        self.GB = self.GW // 128
        self.FH = D // 128
        self.FHC = self.FH // 4
        self.DFF = ((8 * D // 3 + 255) // 256) * 256
        self.NFB = self.DFF // 128
        self.NS = 2 * self.R + self.FHC
        self.KG = self.KC // 4
        self.NBLK = 2 * (2 * self.GB + 2) + 3 * self.FHC + 2 * self.KG
        self.YCB = 2 * self.GB + self.FHC + 2 * self.KG
        self.oZ = 0
        self.oX = self.DI
        self.oB = 2 * self.DI
        self.oC = 2 * self.DI + 1024
        self.oDT = 2 * self.DI + 2048
        self.oQ = self.oDT + self.SH
        self.oK = self.oQ + D
        self.oV = self.oK + D
        self.oF = self.oV + D
        self.oG1 = self.oF + self.FH
        self.oG2 = self.oG1 + D


class Res:
    __slots__ = ("w", "r")

    def __init__(self):
        self.w = None
        self.r = {}


class Sched:
    def __init__(self, nc, stack, ndma=20):
        self.nc = nc
        self.engs = {"pe": nc.tensor, "act": nc.scalar, "dve": nc.vector, "pool": nc.gpsimd, "sp": nc.sync}
        self.prog = {k: [] for k in self.engs}
        self.sems = {}
        for k in self.engs:
            self.sems[k] = stack.enter_context(nc.semaphore("s_" + k))
        self.cnt = {k: 0 for k in self.engs}
        self.ndma = ndma
        self.qring = {"sp": list(range(0, 8)), "act": list(range(8, 16)), "pool": list(range(16, 20))}
        self.qnext = {"sp": 0, "act": 0, "pool": 0}
        for i in range(ndma):
            self.sems[("d", i)] = stack.enter_context(nc.semaphore("d%d" % i))
        self.sems["cc"] = stack.enter_context(nc.semaphore("cc"))
        self.ccval = 0
        self.dval = [0] * ndma
        self.dnobar = [False] * ndma
        self.dnext = 0
        self.seen = {k: {} for k in self.engs}

    def _wait(self, eng, ev):
        if ev is None:
            return
        key, val = ev
        if eng == "pe" and key == "pe":
            return
        if self.seen[eng].get(key, 0) >= val:
            return
        self.seen[eng][key] = val
        s = self.sems[key]
        self.prog[eng].append(lambda e, s=s, v=val: e.wait_ge(s, v))

    def _deps(self, eng, reads, writes):
        for r in reads:
            self._wait(eng, r.w)
        for w in writes:
            self._wait(eng, w.w)
            for k, v in w.r.items():
                self._wait(eng, (k, v))

    def _commit(self, ev, reads, writes):
        for r in reads:
            if r.r.get(ev[0], 0) < ev[1]:
                r.r[ev[0]] = ev[1]
        for w in writes:
            w.w = ev
            w.r = {}

    def op(self, eng, fns, reads=(), writes=()):
        if not isinstance(fns, (list, tuple)):
            fns = [fns]
        self._deps(eng, reads, writes)
        self.cnt[eng] += 1
        ev = (eng, self.cnt[eng])
        for f in fns[:-1]:
            self.prog[eng].append(f)
        s = self.sems[eng]
        self.prog[eng].append(lambda e, f=fns[-1], s=s: f(e).then_inc(s, 1))
        self._commit(ev, reads, writes)

    def dma(self, q, fn, reads=(), writes=(), nobar=False):
        ring = self.qring[q]
        i = ring[self.qnext[q] % len(ring)]
        self.qnext[q] += 1
        key = ("d", i)
        if self.dval[i] > 0:
            self._wait(q, (key, self.dval[i]))
        self._deps(q, reads, writes)
        self.dval[i] += 16
        self.dnobar[i] = nobar
        ev = (key, self.dval[i])
        s = self.sems[key]
        self.prog[q].append(lambda e, f=fn, s=s: f(e).then_inc(s, 16))
        self._commit(ev, reads, writes)

    def coll(self, fn, reads=(), writes=()):
        self._deps("pool", reads, writes)
        self.ccval += 1
        ev = ("cc", self.ccval)
        s = self.sems["cc"]
        self.prog["pool"].append(lambda e, f=fn, s=s: f(e).then_inc(s))
        self._commit(ev, reads, writes)

    def barrier(self, with_cc=False):
        for eng in self.engs:
            for k in self.engs:
                if k != eng and self.cnt[k] > 0:
                    self._wait(eng, (k, self.cnt[k]))
            for i in range(self.ndma):
                if self.dval[i] > 0 and not (self.dnobar[i] and not with_cc):
                    self._wait(eng, (("d", i), self.dval[i]))
            if with_cc and self.ccval > 0:
                self._wait(eng, ("cc", self.ccval))

    def final_wait(self):
        for i in range(self.ndma):
            if self.dval[i] > 0:
                self._wait("sp", (("d", i), self.dval[i]))
        for k in self.engs:
            if k != "sp" and self.cnt[k] > 0:
                self._wait("sp", (k, self.cnt[k]))

    def emit(self):
        nc = self.nc
        prog = self.prog
        with nc.Block() as block:
            @block.tensor
            def _(e):
                for f in prog["pe"]:
                    f(e)

            @block.scalar
            def _(e):
                for f in prog["act"]:
                    f(e)

            @block.vector
            def _(e):
                for f in prog["dve"]:
                    f(e)

            @block.gpsimd
            def _(e):
                for f in prog["pool"]:
                    f(e)

            @block.sync
            def _(e):
                for f in prog["sp"]:
                    f(e)


class Ring:
    def __init__(self, nc, stack, name, n, shape, dtype, nres=1):
        self.t = [stack.enter_context(nc.sbuf_tensor("%s%d" % (name, i), shape, dtype)) for i in range(n)]
        self.res = [[Res() for _ in range(nres)] for _ in range(n)]
        self.i = 0
        self.n = n

    def next(self):
        i = self.i
        self.i = (i + 1) % self.n
        return self.t[i], self.res[i]


def mm(out, lhsT, rhs, start, stop):
    return lambda e: e.matmul(out, lhsT, rhs, start=start, stop=stop)


def build_consts():
    i = np.arange(128)
    same = (i[:, None] // 64) == (i[None, :] // 64)
    c = {}
    c["ident"] = np.eye(128)
    c["MU"] = ((i[:, None] <= i[None, :]) & same)
    c["ML"] = ((i[:, None] > i[None, :]) & same)
    c["HO0"] = np.broadcast_to((i[:, None] < 64), (128, 128))
    c["HO1"] = np.broadcast_to((i[:, None] >= 64), (128, 128))
    c["CM"] = (i[:, None] <= i[None, :])
    c["ONES"] = np.ones((128, 128))
    c["E0"] = np.broadcast_to((i[:, None] == 0), (128, 128))
    c["COL0"] = np.broadcast_to((i[None, :] < 64), (128, 128))
    c["COL1"] = np.broadcast_to((i[None, :] >= 64), (128, 128))
    names = list(c.keys())
    arr = np.stack([np.asarray(c[n], dtype=np.float32) for n in names], axis=1)
    return names, np.ascontiguousarray(arr)


CONST_NAMES, CONST_ARR = build_consts()
NCONST = len(CONST_NAMES)


def build_program(D, stages=("A", "X", "B"), debug=False):
    dm = Dims(D)
    KC, GB, GW, R, FHC, NS, KG, NBLK, YCB, NFB = dm.KC, dm.GB, dm.GW, dm.R, dm.FHC, dm.NS, dm.KG, dm.NBLK, dm.YCB, dm.NFB
    nc = bass.Bass("TRN2", target_bir_lowering=False)
    doA, doX, doB = "A" in stages, "X" in stages, "B" in stages

    in_names = []

    def din(name, shape, dt=F32):
        in_names.append(name)
        return nc.dram_tensor(name, list(shape), dt, kind="ExternalInput").ap()

    def dout(name, shape, dt=F32):
        return nc.dram_tensor(name, list(shape), dt, kind="ExternalOutput").ap()

    def dscr(name, shape, dt, ext=None):
        if ext == "in":
            return din(name, shape, dt)
        if ext == "out":
            return dout(name, shape, dt)
        return nc.dram_tensor(name, list(shape), dt).ap()

    YROWS = YCB * 128
    YB_G, YB_S, YB_A = 0, 2 * KG, 2 * KG + 2 * GB
    LP = L + 2
    consts_d = din("consts", [128, NCONST, 128])
    t0d = din("t0", [1, 3], I32) if doX else None
    if doA:
        xT = din("xT", [D, L])
        wA = din("wA", [NBLK, 128, KC, 128])
        wS = din("wS", [128, KC, NS])
        convw = din("convw", [128, 2 * (GB + 2), 4])
        convb = din("convb", [128, 2 * (GB + 2)])
        dtb = din("dtb", [128, 2 * R])
        alog = din("alog", [128, 2 * R])
        dskip = din("dskip", [128, 2 * GW])
        normw = din("normw", [128, 2 * GW])
        fbias = din("fbias", [128, FHC])
        xTb = dscr("xTb", [D, L], BF16)
        projT = dscr("projT", [NBLK * 128, L], BF16, "out" if (debug and not doB) else None)
    ycat = dscr("ycat", [YROWS, L], BF16, ("out" if not doX else None) if doA else None) if doA else None
    yall = dscr("yall", [4 * YROWS, L], BF16) if doX else None
    yall_dbg = dout("yall_dbg", [4 * YROWS, L], BF16) if (doX and not doB) else None
    ywin = dscr("ywin", [4 * YROWS, 1026], BF16, None if doX else "in") if doB or doX else None
    if doB:
        xTs = din("xTs", [D, 1026])
        hmask = din("hmask", [128, 1])
        gbias = din("gbias", [128, 2, KC])
        wps = din("wps", [KC, 128, 2 * KC, 128])
        wpa = din("wpa", [KC, 128, KC, 128])
        wo = din("wo", [KC, 128, KC, 128])
        wup = din("wup", [2 * NFB, 128, KC, 128])
        wdn = din("wdn", [KC, 128, NFB, 128])
        lng = din("lng", [128, 4, KC])
        fcw = din("fcw", [128, 2 * NFB, 3])
        fcb = din("fcb", [128, 2 * NFB])
        hsc = dscr("hsc", [D, 514], F32)
        h1n = dscr("h1n", [D, 514], F32)
        outT = dout("outT", [D, 1024])

    with contextlib.ExitStack() as top:
        S = Sched(nc, top)
        ps = [top.enter_context(nc.psum_tensor("ps%d" % i, [128, 512], F32)) for i in range(8)]
        psr = [Res() for _ in range(8)]
        cf = top.enter_context(nc.sbuf_tensor("cf", [128, NCONST, 128], F32))
        cb = top.enter_context(nc.sbuf_tensor("cb", [128, NCONST, 128], BF16))
        r_c = Res()
        S.dma("sp", lambda e: e.dma_start(out=cf[:], in_=consts_d[:, :, :]), writes=[r_c])
        S.op("dve", lambda e: e.tensor_copy(cb[:], cf[:]), reads=[r_c], writes=[r_c])

        def CF(n):
            return cf[:, CONST_NAMES.index(n), :]

        def CB(n):
            return cb[:, CONST_NAMES.index(n), :]

        r_xTb, r_proj, r_yall = Res(), Res(), Res()
        r_ycb = [Res() for _ in range(YCB)]
        cstate = {'n': 0}
        NCHK = YCB

        dyn = {}
        r_yw = Res()

        def dv(e, i, q="pool"):
            if (q, i) not in dyn:
                r = top.enter_context(e.register("t0r%s%d" % (q, i)))
                e.reg_load(r, t0d[0:1, i:i + 1])
                dyn[(q, i)] = e.snap(r, min_val=0, max_val=(3072, 3070, 3584)[i])
            return dyn[(q, i)]

        def emit_wcopy(jb0, jb1):
            ya4 = yall.rearrange("(k r p) t -> k r p t", r=4, p=128)
            for r_ in range(4):
                S.dma("pool", lambda e, r_=r_: e.dma_start(
                    out=ywin[r_ * YROWS + jb0 * 128:r_ * YROWS + jb1 * 128, 2:1026].rearrange("(k p) t -> k p t", p=128),
                    in_=ya4[jb0:jb1, r_, :, bass.ds(dv(e, 0), 1024)]), reads=[r_yall], writes=[r_yw], nobar=True)
                S.dma("pool", lambda e, r_=r_: e.dma_start(
                    out=ywin[r_ * YROWS + jb0 * 128:r_ * YROWS + jb1 * 128, 0:2].rearrange("(k p) t -> k p t", p=128),
                    in_=ya4[jb0:jb1, r_, :, bass.ds(dv(e, 1), 2)]), reads=[r_yall], writes=[r_yw], nobar=True)

        def emit_coll(kmax):
            while cstate['n'] < kmax:
                k = cstate['n']
                S.coll(lambda e, k=k: e.collective_compute('AllGather', ALU.bypass, [[0, 1, 2, 3], [4, 5, 6, 7]], ins=[ycat[k * 128:(k + 1) * 128, :].opt()], outs=[yall[k * 512:(k + 1) * 512, :].opt()]),
                       reads=[r_ycb[k]], writes=[r_yall])
                cstate['n'] += 1


        if doA:
            with contextlib.ExitStack() as st:
                for kc in range(KC):
                    S.dma("pool", lambda e, kc=kc: e.dma_start(out=xTb[kc * 128:(kc + 1) * 128, :],
                                                                in_=xT[kc * 128:(kc + 1) * 128, :]), writes=[r_xTb])
                S.barrier()

            stA = contextlib.ExitStack()
            raws = stA.enter_context(nc.sbuf_tensor("raws", [128, NBLKT, NS], F32))
            r_raws = Res()

            gate0 = NBLK - 2 * KG
            with contextlib.ExitStack() as st:
                wring = Ring(nc, st, "wA", 2, [128, 4, KC, 128], BF16)
                xring = Ring(nc, st, "xc", 2, [128, KC, 512], BF16)
                sring = Ring(nc, st, "stg", 3, [128, 512], BF16)
                wsm = st.enter_context(nc.sbuf_tensor("wsm", [128, KC, NS], BF16))
                r_wsm = Res()
                S.dma("pool", lambda e: e.dma_start(out=wsm[:], in_=wS[:, :, :]), writes=[r_wsm])
                units = [list(range(u, min(u + 4, NBLK))) for u in range(0, NBLK, 4)]
                nev = 0
                for ui, unit in enumerate(units):
                    wt, wr = wring.next()
                    for bi, j in enumerate(unit):
                        S.dma("pool", lambda e, wt=wt, bi=bi, j=j: e.dma_start(out=wt[:, bi], in_=wA[j]), writes=wr)
                    for tc in range(8):
                        xt_, xr = xring.next()
                        S.dma("sp", lambda e, xt_=xt_, tc=tc: e.dma_start(
                            out=xt_[:], in_=xTb.rearrange("(k p) t -> p k t", p=128)[:, :, tc * 512:(tc + 1) * 512]),
                            reads=[r_xTb], writes=xr)
                        for bi, j in enumerate(unit):
                            bk = nev % 2
                            S.op("pe", [mm(ps[bk][:], wt[:, bi, kc, :], xt_[:, kc, :], kc == 0, kc == KC - 1)
                                        for kc in range(KC)], reads=wr + xr, writes=[psr[bk]])
                            sg, sr = sring.next()
                            if nev % 2 == 0:
                                S.op("act", lambda e, sg=sg, bk=bk: e.activation(out=sg[:], in_=ps[bk][:], func=AF.Identity),
                                     reads=[psr[bk]], writes=sr)
                            else:
                                S.op("dve", lambda e, sg=sg, bk=bk: e.tensor_copy(sg[:], ps[bk][:]),
                                     reads=[psr[bk]], writes=sr)
                            nev += 1
                            if j < gate0:
                                S.dma("act", lambda e, sg=sg, j=j, tc=tc: e.dma_start(
                                    out=projT[j * 128:(j + 1) * 128, tc * 512:(tc + 1) * 512], in_=sg[:]),
                                    reads=sr, writes=[r_proj])
                            else:
                                yb = YB_G + (j - gate0)
                                S.dma("act", lambda e, sg=sg, yb=yb, tc=tc: e.dma_start(
                                    out=ycat[yb * 128:(yb + 1) * 128, tc * 512:(tc + 1) * 512], in_=sg[:]),
                                    reads=sr, writes=[r_ycb[yb]])
                        if ui == 0:
                            for tb in range(4):
                                blk = tc * 4 + tb
                                S.op("pe", [mm(ps[2][:, 0:NS], xt_[:, kc, tb * 128:(tb + 1) * 128], wsm[:, kc, :], kc == 0, kc == KC - 1)
                                            for kc in range(KC)], reads=xr + [r_wsm], writes=[psr[2]])
                                S.op("dve", lambda e, blk=blk: e.tensor_copy(raws[:, blk, :], ps[2][:, 0:NS]),
                                     reads=[psr[2]], writes=[r_raws])
                S.barrier()
            if doX:
                emit_coll(YB_S)
                emit_wcopy(0, YB_S)

            NH = 2 * R
            NSC = NBLKT * NH
            dt_ = stA.enter_context(nc.sbuf_tensor("dt_", [128, NBLKT, NH], F32))
            da_ = stA.enter_context(nc.sbuf_tensor("da_", [128, NBLKT, NH], F32))
            eacs = stA.enter_context(nc.sbuf_tensor("eacs", [128, NBLKT, NH], F32))
            wend = stA.enter_context(nc.sbuf_tensor("wend", [128, NBLKT, NH], F32))
            cdec = [stA.enter_context(nc.sbuf_tensor("cdec%d" % h, [128, NBLKT, NH], F32)) for h in range(2)]
            Ftok = stA.enter_context(nc.sbuf_tensor("Ftok", [128, NBLKT, FHC], F32))
            Fq = stA.enter_context(nc.sbuf_tensor("Fq", [128, NBLKT, FHC], F32))
            r_sm = Res()
            with contextlib.ExitStack() as st:
                prm = st.enter_context(nc.sbuf_tensor("prm", [128, 4 * R + FHC], F32))
                r_prm = Res()
                S.dma("sp", lambda e: e.dma_start(out=prm[:, 0:NH], in_=dtb[:, :]), writes=[r_prm])
                S.dma("sp", lambda e: e.dma_start(out=prm[:, NH:2 * NH], in_=alog[:, :]), writes=[r_prm])
                S.dma("sp", lambda e: e.dma_start(out=prm[:, 2 * NH:2 * NH + FHC], in_=fbias[:, :]), writes=[r_prm])
                tmp = st.enter_context(nc.sbuf_tensor("smtmp", [128, NBLKT, NH], F32))
                lf = st.enter_context(nc.sbuf_tensor("lf", [128, NBLKT, FHC], F32))
                pref = st.enter_context(nc.sbuf_tensor("pref", [128, NBLKT, FHC], F32))
                an = st.enter_context(nc.sbuf_tensor("an", [128, NH], F32))
                rt = Res()
                S.op("act", lambda e: e.activation(out=an[:], in_=prm[:, NH:2 * NH], func=AF.Exp), reads=[r_prm], writes=[rt])
                S.op("dve", lambda e: e.tensor_scalar_mul(an[:], an[:], -1.0), reads=[rt], writes=[rt])
                S.op("dve", lambda e: e.tensor_tensor(tmp[:], raws[:, :, 0:NH], prm[:, 0:NH].unsqueeze(1).to_broadcast([128, NBLKT, NH]), ALU.add),
                     reads=[r_raws, r_prm], writes=[rt])
                S.op("act", lambda e: e.activation(out=tmp[:], in_=tmp[:], func=AF.Exp), reads=[rt], writes=[rt])
                S.op("act", lambda e: e.activation(out=dt_[:], in_=tmp[:], func=AF.Ln, bias=1.0), reads=[rt], writes=[r_sm])
                S.op("dve", lambda e: e.tensor_tensor(da_[:], dt_[:], an[:].unsqueeze(1).to_broadcast([128, NBLKT, NH]), ALU.mult),
                     reads=[r_sm, rt], writes=[r_sm])
                daf = da_[:].rearrange("p b h -> p (b h)")

                def cum(dst, lhs_name, bank):
                    for c0 in range(0, NSC, 512):
                        c1 = min(NSC, c0 + 512)
                        S.op("pe", mm(ps[bank][:, 0:c1 - c0], CF(lhs_name), daf[:, c0:c1], True, True), reads=[r_sm, r_c], writes=[psr[bank]])
                        S.op("act", lambda e, c0=c0, c1=c1: e.activation(out=dst[:].rearrange("p b h -> p (b h)")[:, c0:c1], in_=ps[bank][:, 0:c1 - c0], func=AF.Exp),
                             reads=[psr[bank]], writes=[r_sm])
                cum(eacs, "MU", 0)
                cum(wend, "ML", 1)
                cum(cdec[0], "HO0", 2)
                cum(cdec[1], "HO1", 3)
                S.op("dve", lambda e: e.tensor_tensor(lf[:], raws[:, :, NH:NH + FHC], prm[:, 2 * NH:2 * NH + FHC].unsqueeze(1).to_broadcast([128, NBLKT, FHC]), ALU.add),
                     reads=[r_raws, r_prm], writes=[rt])
                S.op("act", lambda e: e.activation(out=lf[:], in_=lf[:], func=AF.Exp, scale=-1.0), reads=[rt], writes=[rt])
                S.op("act", lambda e: e.activation(out=lf[:], in_=lf[:], func=AF.Ln, bias=1.0), reads=[rt], writes=[rt])
                S.op("dve", lambda e: e.tensor_scalar_mul(lf[:], lf[:], -1.0), reads=[rt], writes=[rt])
                S.op("dve", lambda e: e.memset(pref[:, 0, :], 0.0), writes=[rt])
                for b_ in range(1, NBLKT):
                    S.op("dve", lambda e, b_=b_: e.tensor_tensor(pref[:, b_, :], pref[:, b_ - 1, :], lf[:, b_ - 1, :], ALU.add), reads=[rt], writes=[rt])
                nf = NBLKT * FHC
                S.op("pe", [mm(ps[4][:, 0:nf], CF("CM"), lf[:].rearrange("p b h -> p (b h)"), True, False),
                            mm(ps[4][:, 0:nf], CF("ONES"), pref[:].rearrange("p b h -> p (b h)"), False, True)],
                     reads=[rt, r_c], writes=[psr[4]])
                S.op("dve", lambda e: e.tensor_copy(Ftok[:].rearrange("p b h -> p (b h)"), ps[4][:, 0:nf]), reads=[psr[4]], writes=[r_sm])
                S.op("pe", mm(ps[5][:, 0:nf], CF("E0"), Ftok[:].rearrange("p b h -> p (b h)"), True, True), reads=[r_sm, r_c], writes=[psr[5]])
                S.op("dve", lambda e: e.tensor_copy(Fq[:].rearrange("p b h -> p (b h)"), ps[5][:, 0:nf]), reads=[psr[5]], writes=[r_sm])
                S.barrier()

            NCH = min(512, GW)
            NHH = GW // NCH
            HPH = NCH // 64
            with contextlib.ExitStack() as st:
                cwt = st.enter_context(nc.sbuf_tensor("cwt", [128, 2 * (GB + 2), 4], F32))
                cbt = st.enter_context(nc.sbuf_tensor("cbt", [128, 2 * (GB + 2)], F32))
                dsk = st.enter_context(nc.sbuf_tensor("dsk", [128, 2 * GW], F32))
                nrw = st.enter_context(nc.sbuf_tensor("nrw", [128, 2 * GW], F32))
                r_p = Res()
                S.dma("sp", lambda e: e.dma_start(out=cwt[:], in_=convw[:, :, :]), writes=[r_p])
                S.dma("sp", lambda e: e.dma_start(out=cbt[:], in_=convb[:, :]), writes=[r_p])
                S.dma("sp", lambda e: e.dma_start(out=dsk[:], in_=dskip[:, :]), writes=[r_p])
                S.dma("sp", lambda e: e.dma_start(out=nrw[:], in_=normw[:, :]), writes=[r_p])
                rawr = Ring(nc, st, "raw", 2, [128, GB + 2, 515], BF16)
                zr_ = Ring(nc, st, "zraw", 1, [128, GB, 512], BF16)
                accr = Ring(nc, st, "acc", 2, [128, 512], F32)
                xsr = Ring(nc, st, "xs", 1, [128, GB + 2, 512], BF16)
                szr = Ring(nc, st, "sz", 1, [128, GB, 512], BF16)
                H = [st.enter_context(nc.sbuf_tensor("H%d" % g, [128, GW], F32)) for g in range(2)]
                Hb = [[st.enter_context(nc.sbuf_tensor("Hb%d_%d" % (g, i), [128, GW], BF16)) for i in range(2)] for g in range(2)]
                r_H = [Res(), Res()]
                r_Hb = [[Res(), Res()], [Res(), Res()]]
                btk = Ring(nc, st, "btk", 2, [128, 128], BF16)
                xdtr = Ring(nc, st, "xdt", 2, [128, GW], BF16)
                ur = Ring(nc, st, "u", 2, [128, GW], BF16)
                xDr = Ring(nc, st, "xD", 2, [128, GW], F32)
                cbm = Ring(nc, st, "cbm", 2, [128, 128], F32)
                c01 = Ring(nc, st, "c01", 2, [128, 2, 128], BF16)
                lhr = Ring(nc, st, "lh", 2, [128, R, 128], F32)
                er = Ring(nc, st, "E", 2, [128, 512], F32)
                mtr = Ring(nc, st, "MT", 2, [128, 4, 128], BF16)
                t1r = Ring(nc, st, "t1", 2, [128, NCH], F32)
                yzr = Ring(nc, st, "yz", 2, [128, GW], F32)
                sqr = Ring(nc, st, "sq", 1, [128, GW], F32)
                ssr = Ring(nc, st, "ss", 2, [128, 2], F32)
                ynr = Ring(nc, st, "yn", 2, [128, GW], BF16)
                ysr = Ring(nc, st, "ysg", 1, [128, GB, 512], BF16)
                for g in range(2):
                    S.op("dve", lambda e, g=g: e.memset(H[g][:], 0.0), writes=[r_H[g]])
                    S.op("dve", lambda e, g=g: e.memset(Hb[g][0][:], 0.0), writes=[r_Hb[g][0]])
                hbi = [0, 0]
                for sc in range(8):
                    for g in range(2):
                        jx = g * (2 * GB + 2)
                        raw, rr_ = rawr.next()
                        if sc == 0:
                            S.op("dve", lambda e, raw=raw: e.memset(raw[:, :, 0:3], 0.0), writes=rr_)
                            S.dma("sp", lambda e, raw=raw, jx=jx: e.dma_start(out=raw[:, :, 3:515], in_=projT[jx * 128:(jx + GB + 2) * 128, 0:512].rearrange("(j p) t -> p j t", p=128)),
                                  reads=[r_proj], writes=rr_)
                        else:
                            S.dma("sp", lambda e, raw=raw, jx=jx, sc=sc: e.dma_start(out=raw[:], in_=projT[jx * 128:(jx + GB + 2) * 128, sc * 512 - 3:sc * 512 + 512].rearrange("(j p) t -> p j t", p=128)),
                                  reads=[r_proj], writes=rr_)
                        zraw, zrr = zr_.next()
                        jz = jx + GB + 2
                        S.dma("sp", lambda e, zraw=zraw, jz=jz, sc=sc: e.dma_start(out=zraw[:], in_=projT[jz * 128:(jz + GB) * 128, sc * 512:(sc + 1) * 512].rearrange("(j p) t -> p j t", p=128)),
                              reads=[r_proj], writes=zrr)
                        xs, xsres = xsr.next()
                        sz, szres = szr.next()
                        for b_ in range(GB + 2):
                            acc, ar = accr.next()
                            ci = g * (GB + 2) + b_
                            S.op("act", lambda e, acc=acc, raw=raw, b_=b_, ci=ci: e.activation(out=acc[:], in_=raw[:, b_, 3:515], func=AF.Identity,
                                                                                        bias=cbt[:, ci:ci + 1], scale=cwt[:, ci, 3:4]), reads=rr_ + [r_p], writes=ar)
                            for k in (2, 1, 0):
                                S.op("dve", lambda e, acc=acc, raw=raw, b_=b_, ci=ci, k=k: e.scalar_tensor_tensor(
                                    out=acc[:], in0=raw[:, b_, k:k + 512], scalar=cwt[:, ci, k:k + 1], in1=acc[:], op0=ALU.mult, op1=ALU.add),
                                    reads=rr_ + [r_p] + ar, writes=ar)
                            S.op("act", lambda e, acc=acc, xs=xs, b_=b_: e.activation(out=xs[:, b_, :], in_=acc[:], func=AF.Silu), reads=ar, writes=xsres)
                        S.op("act", lambda e, sz=sz, zraw=zraw: e.activation(out=sz[:], in_=zraw[:], func=AF.Silu), reads=zrr, writes=szres)
                        ysg, ysres = ysr.next()
                        def front(tb, g=g, sc=sc, xs=xs, xsres=xsres):
                            blk = sc * 4 + tb
                            tsl = slice(tb * 128, (tb + 1) * 128)
                            hsl = slice(g * R, (g + 1) * R)
                            pxb = ps[0][:].bitcast(BF16)
                            S.op("pe", [lambda e, b_=b_, xs=xs, tsl=tsl, pxb=pxb: e.transpose(pxb[:, b_ * 128:(b_ + 1) * 128], xs[:, b_, tsl], CB("ident")) for b_ in range(GB)],
                                 reads=xsres + [r_c], writes=[psr[0]])
                            pbb = ps[1][:].bitcast(BF16)
                            S.op("pe", lambda e, xs=xs, tsl=tsl, pbb=pbb: e.transpose(pbb[:, 0:128], xs[:, GB, tsl], CB("ident")), reads=xsres + [r_c], writes=[psr[1]])
                            bt, btr = btk.next()
                            S.op("dve", lambda e, bt=bt, pbb=pbb: e.tensor_copy(bt[:], pbb[:, 0:128]), reads=[psr[1]], writes=btr)
                            xdt, xdr = xdtr.next()
                            u_, u_r = ur.next()
                            xD, xDres = xDr.next()
                            px3 = pxb[:, 0:GW].rearrange("p (r c) -> p r c", c=64)
                            S.op("dve", lambda e, xdt=xdt, px3=px3, blk=blk, hsl=hsl: e.tensor_tensor(xdt[:].rearrange("p (r c) -> p r c", c=64), px3,
                                                                                               dt_[:, blk, hsl].unsqueeze(2).to_broadcast([128, R, 64]), ALU.mult),
                                 reads=[psr[0], r_sm], writes=xdr)
                            S.op("dve", lambda e, xdt=xdt, u_=u_, blk=blk, hsl=hsl: e.tensor_tensor(u_[:].rearrange("p (r c) -> p r c", c=64), xdt[:].rearrange("p (r c) -> p r c", c=64),
                                                                                              wend[:, blk, hsl].unsqueeze(2).to_broadcast([128, R, 64]), ALU.mult),
                                 reads=xdr + [r_sm], writes=u_r)
                            S.op("dve", lambda e, xD=xD, pxb=pxb, g=g: e.tensor_tensor(xD[:], pxb[:, 0:GW], dsk[:, g * GW:(g + 1) * GW], ALU.mult),
                                 reads=[psr[0], r_p], writes=xDres)
                            S.op("pe", mm(ps[1][:, 128:256], xs[:, GB, tsl], xs[:, GB + 1, tsl], True, True), reads=xsres + [psr[1]], writes=[psr[1]])
                            cm_, cmr = cbm.next()
                            S.op("dve", lambda e, cm_=cm_: e.tensor_tensor(cm_[:], ps[1][:, 128:256], CF("MU"), ALU.mult), reads=[psr[1], r_c], writes=cmr)
                            cc_, ccr = c01.next()
                            S.op("dve", lambda e, cc_=cc_, xs=xs, tsl=tsl: e.tensor_tensor(cc_[:, 0, :], xs[:, GB + 1, tsl], CB("COL0"), ALU.mult), reads=xsres + [r_c], writes=ccr)
                            S.op("dve", lambda e, cc_=cc_, xs=xs, tsl=tsl: e.tensor_tensor(cc_[:, 1, :], xs[:, GB + 1, tsl], CB("COL1"), ALU.mult), reads=xsres + [r_c] + ccr, writes=ccr)
                            lh, lhres = lhr.next()
                            S.op("dve", lambda e, lh=lh, blk=blk, hsl=hsl: e.tensor_tensor(lh[:], CF("ML").unsqueeze(1).to_broadcast([128, R, 128]),
                                                                                     da_[:, blk, hsl].unsqueeze(2).to_broadcast([128, R, 128]), ALU.mult),
                                 reads=[r_c, r_sm], writes=lhres)
                            return (bt, btr, xdt, xdr, u_, u_r, xD, xDres, cm_, cmr, cc_, ccr, lh, lhres)

                        def back(tb, FR, g=g, sc=sc, xs=xs, xsres=xsres, sz=sz, szres=szres, ysg=ysg, ysres=ysres):
                            (bt, btr, xdt, xdr, u_, u_r, xD, xDres, cm_, cmr, cc_, ccr, lh, lhres) = FR
                            blk = sc * 4 + tb
                            tsl = slice(tb * 128, (tb + 1) * 128)
                            hsl = slice(g * R, (g + 1) * R)
                            yz, yzres = yzr.next()
                            for hp in range(2):
                                cur = hbi[g]
                                nxt = 1 - cur
                                for hf in range(NHH):
                                    csl = slice(hf * NCH, (hf + 1) * NCH)
                                    S.op("pe", mm(ps[4 + hf][:, 0:NCH], cc_[:, hp, :], Hb[g][cur][:, csl], hp == 0, hp == 1),
                                         reads=ccr + [r_Hb[g][cur]], writes=[psr[4 + hf]])
                                    S.op("pe", mm(ps[6 + hf][:, 0:NCH], bt[hp * 64:(hp + 1) * 64, :], u_[hp * 64:(hp + 1) * 64, csl], True, True),
                                         reads=btr + u_r, writes=[psr[6 + hf]])
                                S.op("dve", lambda e, g=g, blk=blk, hsl=hsl, hp=hp: e.tensor_tensor(H[g][:].rearrange("p (r c) -> p r c", c=64), H[g][:].rearrange("p (r c) -> p r c", c=64),
                                                                                                cdec[hp][:, blk, hsl].unsqueeze(2).to_broadcast([128, R, 64]), ALU.mult),
                                     reads=[r_sm, r_H[g]], writes=[r_H[g]])
                                for hf in range(NHH):
                                    csl = slice(hf * NCH, (hf + 1) * NCH)
                                    S.op("dve", lambda e, g=g, csl=csl, hf=hf: e.tensor_tensor(H[g][:, csl], H[g][:, csl], ps[6 + hf][:, 0:NCH], ALU.add),
                                         reads=[psr[6 + hf], r_H[g]], writes=[r_H[g]])
                                S.op("act", lambda e, g=g, nxt=nxt: e.activation(out=Hb[g][nxt][:], in_=H[g][:], func=AF.Identity), reads=[r_H[g]], writes=[r_Hb[g][nxt]])
                                hbi[g] = nxt
                            for hf in range(NHH):
                                csl = slice(hf * NCH, (hf + 1) * NCH)
                                t1, t1res = t1r.next()
                                hs2 = slice(g * R + hf * HPH, g * R + (hf + 1) * HPH)
                                S.op("dve", lambda e, t1=t1, hf=hf, blk=blk, hs2=hs2: e.tensor_tensor(t1[:].rearrange("p (r c) -> p r c", c=64), ps[4 + hf][:, 0:NCH].rearrange("p (r c) -> p r c", c=64),
                                                                                                 eacs[:, blk, hs2].unsqueeze(2).to_broadcast([128, HPH, 64]), ALU.mult),
                                     reads=[psr[4 + hf], r_sm], writes=t1res)
                                for q0 in range(0, HPH, 4):
                                    nq = min(4, HPH - q0)
                                    S.op("pe", [mm(ps[2][:, i * 128:(i + 1) * 128], lh[:, hf * HPH + q0 + i, :], CF("MU"), True, True) for i in range(nq)],
                                         reads=lhres + [r_c], writes=[psr[2]])
                                    E_, Er = er.next()
                                    S.op("act", lambda e, E_=E_, nq=nq: e.activation(out=E_[:, 0:nq * 128], in_=ps[2][:, 0:nq * 128], func=AF.Exp), reads=[psr[2]], writes=Er)
                                    mt, mtres = mtr.next()
                                    S.op("dve", lambda e, mt=mt, E_=E_, cm_=cm_, nq=nq: e.tensor_tensor(mt[:, 0:nq, :], E_[:, 0:nq * 128].rearrange("p (r c) -> p r c", c=128),
                                                                                                  cm_[:].unsqueeze(1).to_broadcast([128, nq, 128]), ALU.mult),
                                         reads=Er + cmr, writes=mtres)
                                    S.op("pe", [mm(ps[3][:, (q0 + i) * 64:(q0 + i + 1) * 64], mt[:, i, :], xdt[:, hf * NCH + (q0 + i) * 64:hf * NCH + (q0 + i + 1) * 64], True, True) for i in range(nq)],
                                         reads=mtres + xdr, writes=[psr[3]])
                                S.op("dve", lambda e, t1=t1: e.tensor_tensor(t1[:], t1[:], ps[3][:, 0:NCH], ALU.add), reads=[psr[3]] + t1res, writes=t1res)
                                S.op("dve", lambda e, t1=t1, xD=xD, csl=csl: e.tensor_tensor(t1[:], t1[:], xD[:, csl], ALU.add), reads=xDres + t1res, writes=t1res)
                                pzb = ps[0][:].bitcast(BF16)
                                nbz = NCH // 128
                                S.op("pe", [lambda e, i=i, sz=sz, tsl=tsl, pzb=pzb, hf=hf, nbz=nbz: e.transpose(pzb[:, i * 128:(i + 1) * 128], sz[:, hf * nbz + i, tsl], CB("ident")) for i in range(nbz)],
                                     reads=szres + [r_c, psr[0]], writes=[psr[0]])
                                S.op("dve", lambda e, yz=yz, t1=t1, csl=csl, pzb=pzb: e.tensor_tensor(yz[:, csl], t1[:], pzb[:, 0:NCH], ALU.mult), reads=[psr[0]] + t1res, writes=yzres)
                            sq, sqres = sqr.next()
                            ss, ssres = ssr.next()
                            S.op("dve", lambda e, ss=ss: e.memset(ss[:], 0.0), writes=ssres)
                            S.op("act", lambda e, sq=sq, yz=yz, ss=ss: e.activation(out=sq[:], in_=yz[:], func=AF.Square, accum_out=ss[:, 0:1]), reads=yzres + ssres, writes=sqres + ssres)
                            S.op("dve", lambda e, ss=ss: e.tensor_scalar(ss[:, 1:2], ss[:, 0:1], 1.0 / GW, RMS_EPS, ALU.mult, ALU.add), reads=ssres, writes=ssres)
                            S.op("act", lambda e, ss=ss: e.activation(out=ss[:, 1:2], in_=ss[:, 1:2], func=AF.Sqrt), reads=ssres, writes=ssres)
                            S.op("dve", lambda e, ss=ss: e.reciprocal(ss[:, 1:2], ss[:, 1:2]), reads=ssres, writes=ssres)
                            yn, ynres = ynr.next()
                            S.op("dve", lambda e, yn=yn, yz=yz, ss=ss, g=g: e.scalar_tensor_tensor(out=yn[:], in0=yz[:], scalar=ss[:, 1:2], in1=nrw[:, g * GW:(g + 1) * GW], op0=ALU.mult, op1=ALU.mult),
                                 reads=yzres + ssres + [r_p], writes=ynres)
                            pyb = ps[2][:].bitcast(BF16)
                            S.op("pe", [lambda e, b_=b_, yn=yn, pyb=pyb: e.transpose(pyb[:, b_ * 128:(b_ + 1) * 128], yn[:, b_ * 128:(b_ + 1) * 128], CB("ident")) for b_ in range(GB)],
                                 reads=ynres + [r_c], writes=[psr[2]])
                            S.op("act", lambda e, ysg=ysg, pyb=pyb, tsl=tsl: e.activation(out=ysg[:, :, tsl], in_=pyb[:, 0:GW].rearrange("p (b c) -> p b c", c=128), func=AF.Identity),
                                 reads=[psr[2]], writes=ysres)
                        fr_ = front(0)
                        for tb in range(4):
                            nfr = front(tb + 1) if tb < 3 else None
                            back(tb, fr_)
                            fr_ = nfr
                        S.dma("act", lambda e, ysg=ysg, g=g, sc=sc: e.dma_start(out=ycat[YB_S * 128 + g * GW:YB_S * 128 + (g + 1) * GW, sc * 512:(sc + 1) * 512].rearrange("(j p) t -> p j t", p=128), in_=ysg[:]),
                              reads=ysres, writes=[r_ycb[YB_S + g * GB + b_] for b_ in range(GB)])
                S.barrier()
            if doX:
                emit_coll(YB_A)
            scale = 1.0 / np.sqrt(128.0)
            with contextlib.ExitStack() as st:
                qkv = Ring(nc, st, "qkv", 2, [128, 3, L], BF16)
                vtok = st.enter_context(nc.sbuf_tensor("vtok", [128, NBLKT, 128], BF16))
                r_vtok = Res()
                boff = st.enter_context(nc.sbuf_tensor("boff", [128, 8, NBLKT], F32))
                bdg = st.enter_context(nc.sbuf_tensor("bdg", [128, 8, 4, 4], F32))
                gq = st.enter_context(nc.sbuf_tensor("gq", [128, NBLKT], F32))
                r_btab = Res()
                ptr = Ring(nc, st, "pt", 4, [128, 512], BF16)
                rin = Ring(nc, st, "rinv", 4, [128, 512], F32)
                yst = Ring(nc, st, "yst", 8, [128, 512], BF16)
                hb0 = 2 * (2 * GB + 2)
                for hh in range(FHC):
                    qt, qr = qkv.next()
                    j0 = hb0 + 3 * hh
                    S.dma("sp", lambda e, qt=qt, j0=j0: e.dma_start(out=qt[:], in_=projT[j0 * 128:(j0 + 3) * 128, :].rearrange("(j p) t -> p j t", p=128)),
                          reads=[r_proj], writes=qr)
                    for g4 in range(NBLKT // 8):
                        pv = ps[2][:].bitcast(BF16)
                        S.op("pe", [lambda e, g4=g4, i=i, pv=pv, qt=qt: e.transpose(pv[:, i * 128:(i + 1) * 128], qt[:, 2, (g4 * 8 + i) * 128:(g4 * 8 + i + 1) * 128], CB("ident"))
                                    for i in range(8)], reads=qr + [r_c], writes=[psr[2]])
                        S.op("dve", lambda e, g4=g4, pv=pv: e.tensor_copy(vtok[:, g4 * 8:(g4 + 1) * 8, :].rearrange("p b d -> p (b d)"), pv),
                             reads=[psr[2]], writes=[r_vtok])
                    Fq4 = Fq[:, :, hh].rearrange("p (c j) -> p c j", j=4)
                    Ft4 = Ftok[:, :, hh].rearrange("p (c j) -> p c j", j=4)
                    S.op("dve", lambda e, hh=hh, Fq4=Fq4: e.tensor_tensor(boff[:], Fq4[:, :, 0:1].to_broadcast([128, 8, NBLKT]),
                                                                      Ftok[:, :, hh].unsqueeze(1).to_broadcast([128, 8, NBLKT]), ALU.subtract),
                         reads=[r_sm], writes=[r_btab])
                    S.op("dve", lambda e, Fq4=Fq4, Ft4=Ft4: e.tensor_tensor(bdg[:], Fq4.unsqueeze(2).to_broadcast([128, 8, 4, 4]),
                                                                        Ft4.unsqueeze(3).to_broadcast([128, 8, 4, 4]), ALU.subtract),
                         reads=[r_sm], writes=[r_btab])
                    S.op("dve", lambda e, Fq4=Fq4: e.tensor_tensor(gq[:].rearrange("p (c j) -> p c j", j=4), Fq4, Fq4[:, :, 0:1].to_broadcast([128, 8, 4]), ALU.subtract),
                         reads=[r_sm], writes=[r_btab])
                    S.op("act", lambda e: e.activation(out=gq[:], in_=gq[:], func=AF.Exp), reads=[r_btab], writes=[r_btab])
                    pairs = [(c, kb) for c in range(8) for kb in range(4 * c + 4)]
                    sbanks = [0, 1, 5]
                    LOOK = 2

                    def emit_S(n):
                        c, kb = pairs[n]
                        sb = sbanks[n % 3]
                        S.op("pe", mm(ps[sb][:], qt[:, 1, kb * 128:(kb + 1) * 128], qt[:, 0, c * 512:(c + 1) * 512], True, True),
                             reads=qr, writes=[psr[sb]])

                    for n in range(min(LOOK, len(pairs))):
                        emit_S(n)
                    for n, (c, kb) in enumerate(pairs):
                        if n + LOOK < len(pairs):
                            emit_S(n + LOOK)
                        nkb = 4 * c + 4
                        sb = sbanks[n % 3]
                        pt, pr = ptr.next()
                        if kb < 4 * c:
                            S.op("act", lambda e, pt=pt, sb=sb, c=c, kb=kb: e.activation(out=pt[:], in_=ps[sb][:], func=AF.Exp,
                                                                                  bias=boff[:, c, kb:kb + 1], scale=float(scale)), reads=[psr[sb], r_btab], writes=pr)
                            S.op("pe", [mm(ps[3][:], vtok[:, kb, :], pt[:], kb == 0, kb == 4 * c - 1),
                                        mm(ps[4][:], CB("ONES"), pt[:], kb == 0, kb == 4 * c - 1)],
                                 reads=pr + [r_vtok, r_c], writes=[psr[3], psr[4]])
                        else:
                            jd = kb - 4 * c
                            for j in range(jd, 4):
                                S.op("act", lambda e, pt=pt, sb=sb, j=j, c=c, jd=jd: e.activation(
                                    out=pt[:, j * 128:(j + 1) * 128], in_=ps[sb][:, j * 128:(j + 1) * 128], func=AF.Exp,
                                    bias=bdg[:, c, jd, j:j + 1], scale=float(scale)), reads=[psr[sb], r_btab], writes=pr)
                            S.op("dve", lambda e, pt=pt, jd=jd: e.tensor_tensor(pt[:, jd * 128:(jd + 1) * 128], pt[:, jd * 128:(jd + 1) * 128], CB("CM"), ALU.mult),
                                 reads=pr + [r_c], writes=pr)
                            S.op("pe", [mm(ps[6][:, jd * 128:512], vtok[:, kb, :], pt[:, jd * 128:512], jd == 0, jd == 3),
                                        mm(ps[7][:, jd * 128:512], CB("ONES"), pt[:, jd * 128:512], jd == 0, jd == 3)],
                                 reads=pr + [r_vtok, r_c], writes=[psr[6], psr[7]])
                        if kb == nkb - 1:
                            ri, rr = rin.next()
                            ys, yr = yst.next()
                            if c == 0:
                                S.op("dve", lambda e, ri=ri: e.reciprocal(ri[:], ps[7][:]), reads=[psr[7]], writes=rr)
                                S.op("dve", lambda e, ri=ri, ys=ys: e.tensor_tensor(ys[:], ps[6][:], ri[:], ALU.mult), reads=[psr[6]] + rr, writes=yr)
                            else:
                                to, tor = rin.next()
                                gbc = gq[:, 4 * c:4 * c + 4].unsqueeze(2).to_broadcast([128, 4, 128])
                                S.op("dve", lambda e, ri=ri, gbc=gbc: e.tensor_tensor(ri[:].rearrange("p (j q) -> p j q", q=128), ps[4][:].rearrange("p (j q) -> p j q", q=128), gbc, ALU.mult),
                                     reads=[psr[4], r_btab], writes=rr)
                                S.op("dve", lambda e, ri=ri: e.tensor_tensor(ri[:], ri[:], ps[7][:], ALU.add), reads=[psr[7]] + rr, writes=rr)
                                S.op("dve", lambda e, ri=ri: e.reciprocal(ri[:], ri[:]), reads=rr, writes=rr)
                                S.op("dve", lambda e, to=to, gbc=gbc: e.tensor_tensor(to[:].rearrange("p (j q) -> p j q", q=128), ps[3][:].rearrange("p (j q) -> p j q", q=128), gbc, ALU.mult),
                                     reads=[psr[3], r_btab], writes=tor)
                                S.op("dve", lambda e, to=to: e.tensor_tensor(to[:], to[:], ps[6][:], ALU.add), reads=[psr[6]] + tor, writes=tor)
                                S.op("dve", lambda e, to=to, ri=ri, ys=ys: e.tensor_tensor(ys[:], to[:], ri[:], ALU.mult), reads=tor + rr, writes=yr)
                            yb = YB_A + hh
                            S.dma("act", lambda e, ys=ys, yb=yb, c=c: e.dma_start(out=ycat[yb * 128:(yb + 1) * 128, c * 512:(c + 1) * 512], in_=ys[:]),
                                  reads=yr, writes=[r_ycb[yb]])
                    if doX:
                        emit_coll(YB_A + hh + 1)
                S.barrier()

            stA.close()

        if doX:
            emit_coll(NCHK)
            if yall_dbg is not None:
                for r_ in range(4):
                    S.dma("sp", lambda e, r_=r_: e.dma_start(out=yall_dbg[r_ * YROWS:(r_ + 1) * YROWS, :], in_=yall[r_ * YROWS:(r_ + 1) * YROWS, :]), reads=[r_yall], writes=[Res()])
            S.barrier(with_cc=True)

        if doB:
            KY = 4 * (2 * GB + FHC)
            NBR = 2 * GB + FHC
            kyblocks = [r * YCB + YB_S + i for r in range(4) for i in range(NBR)]
            ssd_pos = [r * NBR + i for r in range(4) for i in range(2 * GB)]
            att_pos = [r * NBR + 2 * GB + i for r in range(4) for i in range(FHC)]
            with contextlib.ExitStack() as st:
                yw = ywin.rearrange("(j p) t -> p j t", p=128)
                prmB = st.enter_context(nc.sbuf_tensor("prmB", [128, 6, KC], F32))
                fcwt = st.enter_context(nc.sbuf_tensor("fcwt", [128, 2 * NFB, 3], F32))
                fcbt = st.enter_context(nc.sbuf_tensor("fcbt", [128, 2 * NFB], F32))
                hmk = st.enter_context(nc.sbuf_tensor("hmk", [128, 1], F32))
                usave = st.enter_context(nc.sbuf_tensor("usave", [128, 2 * NFB, 2], F32))
                r_pb, r_us = Res(), Res()
                S.dma("sp", lambda e: e.dma_start(out=prmB[:, 0:2, :], in_=gbias[:, :, :]), writes=[r_pb])
                S.dma("sp", lambda e: e.dma_start(out=prmB[:, 2:6, :], in_=lng[:, :, :]), writes=[r_pb])
                S.dma("sp", lambda e: e.dma_start(out=fcwt[:], in_=fcw[:, :, :]), writes=[r_pb])
                S.dma("sp", lambda e: e.dma_start(out=fcbt[:], in_=fcb[:, :]), writes=[r_pb])
                S.dma("sp", lambda e: e.dma_start(out=hmk[:], in_=hmask[:, :]), writes=[r_pb])
                wring = Ring(nc, st, "wB", 4, [128, 32, 128], BF16)
                tmpr = Ring(nc, st, "tB", 6, [128, 514], F32)
                gring = Ring(nc, st, "gB", 4, [128, 2, 514], BF16)
                mean = st.enter_context(nc.sbuf_tensor("mean", [128, 514], F32))
                rstd = st.enter_context(nc.sbuf_tensor("rstd", [128, 514], F32))
                r_st = Res()
                yT = st.enter_context(nc.sbuf_tensor("yT", [128, max(KY, NFB), 514], BF16))
                mrg = st.enter_context(nc.sbuf_tensor("mrg", [128, KC, 514], BF16))
                r_yT, r_mrg = Res(), Res()
                r_hsc, r_h1n, r_out = Res(), Res(), Res()

                def load_piece(src, k0, kn):
                    wt, wr = wring.next()
                    S.dma("pool", lambda e, wt=wt: e.dma_start(out=wt[:, 0:kn, :], in_=src[:, k0:k0 + kn, :]), writes=wr)
                    return wt, wr

                def gemm(segs, pieces, act_fn, act_res, bank_m, bank_h):
                    tot = sum(p[2] for p in pieces)
                    done = 0
                    for (src, k0, kn, kidx) in pieces:
                        wt, wr = load_piece(src, k0, kn)
                        fl = []
                        for i in range(kn):
                            for (c0, c1, isold) in segs:
                                bank = bank_h if isold else bank_m
                                o = ps[bank][:, 0:c1 - c0]
                                fl.append(mm(o, wt[:, i, :], act_fn(kidx[i], c0, c1), done + i == 0, done + i == tot - 1))
                        S.op("pe", fl, reads=wr + act_res, writes=[psr[bank_m]] + ([psr[bank_h]] if len(segs) > 1 else []))
                        done += kn

                def ln_stats_acc(segs, src_tile, src_res, j):
                    sqt, sqres = tmpr.next()
                    S.op("act", lambda e: e.activation(out=sqt[:, 0:segs[-1][1]], in_=src_tile[:, 0:segs[-1][1]], func=AF.Square), reads=src_res, writes=sqres)
                    fl = []
                    for (c0, c1, ish) in segs:
                        if not ish:
                            fl.append(mm(ps[4][:, 0:c1 - c0], CF("ONES"), src_tile[:, c0:c1], j == 0, j == KC - 1))
                            fl.append(mm(ps[5][:, 0:c1 - c0], CF("ONES"), sqt[:, c0:c1], j == 0, j == KC - 1))
                        else:
                            fl.append(mm(ps[6][:, 0:2], CF("ONES"), src_tile[:, c0:c1], j == 0, j == KC - 1))
                            fl.append(mm(ps[6][:, 2:4], CF("ONES"), sqt[:, c0:c1], j == 0, j == KC - 1))
                    S.op("pe", fl, reads=src_res + sqres + [r_c], writes=[psr[4], psr[5], psr[6]])

                def ln_finish(segs):
                    W = segs[-1][1]
                    for (c0, c1, ish) in segs:
                        sm = ps[6][:, 0:2] if ish else ps[4][:, 0:c1 - c0]
                        sq_ = ps[6][:, 2:4] if ish else ps[5][:, 0:c1 - c0]
                        S.op("dve", lambda e, c0=c0, c1=c1, sm=sm: e.tensor_scalar_mul(mean[:, c0:c1], sm, 1.0 / D), reads=[psr[4], psr[6]], writes=[r_st])
                        S.op("dve", lambda e, c0=c0, c1=c1, sq_=sq_: e.tensor_scalar_mul(rstd[:, c0:c1], sq_, 1.0 / D), reads=[psr[5], psr[6]], writes=[r_st])
                    t_, tr = tmpr.next()
                    S.op("dve", lambda e: e.tensor_tensor(t_[:, 0:W], mean[:, 0:W], mean[:, 0:W], ALU.mult), reads=[r_st], writes=tr)
                    S.op("dve", lambda e: e.tensor_tensor(rstd[:, 0:W], rstd[:, 0:W], t_[:, 0:W], ALU.subtract), reads=[r_st] + tr, writes=[r_st])
                    S.op("dve", lambda e: e.tensor_scalar_add(rstd[:, 0:W], rstd[:, 0:W], LN_EPS), reads=[r_st], writes=[r_st])
                    S.op("act", lambda e: e.activation(out=rstd[:, 0:W], in_=rstd[:, 0:W], func=AF.Sqrt), reads=[r_st], writes=[r_st])
                    S.op("dve", lambda e: e.reciprocal(rstd[:, 0:W], rstd[:, 0:W]), reads=[r_st], writes=[r_st])

                for pi in range(2):
                    segs = [(0, 512, False), (512, 514, True)] if pi == 0 else [(0, 512, False)]
                    W = segs[-1][1]
                    col0 = 2 + pi * 512
                    if doX:
                        ya4 = yall.rearrange("(k r p) t -> k r p t", r=4, p=128)
                        for r_ in range(4):
                            S.dma("sp", lambda e, r_=r_, pi=pi: e.dma_start(
                                out=yT[:, r_ * NBR:(r_ + 1) * NBR, 0:512],
                                in_=ya4[YB_S:YCB, r_, :, bass.ds(dv(e, 2 * pi, "sp"), 512)].rearrange("j p t -> p j t")),
                                reads=[r_yall], writes=[r_yT])
                            if pi == 0:
                                S.dma("sp", lambda e, r_=r_: e.dma_start(
                                    out=yT[:, r_ * NBR:(r_ + 1) * NBR, 512:514],
                                    in_=ya4[YB_S:YCB, r_, :, bass.ds(dv(e, 1, "sp"), 2)].rearrange("j p t -> p j t")),
                                    reads=[r_yall], writes=[r_yT])
                    else:
                        for r_ in range(4):
                            src0 = r_ * YCB + YB_S
                            S.dma("sp", lambda e, r_=r_, src0=src0, pi=pi: e.dma_start(
                                out=yT[:, r_ * NBR:(r_ + 1) * NBR, 0:512], in_=yw[:, src0:src0 + NBR, 2 + pi * 512:2 + (pi + 1) * 512]),
                                reads=[r_yw], writes=[r_yT])
                            if pi == 0:
                                S.dma("sp", lambda e, r_=r_, src0=src0: e.dma_start(
                                    out=yT[:, r_ * NBR:(r_ + 1) * NBR, 512:514], in_=yw[:, src0:src0 + NBR, 0:2]),
                                    reads=[r_yw], writes=[r_yT])
                    nss = len(ssd_pos)
                    for j in range(KC):
                        gts = []
                        for gi in range(2):
                            gt, gr = gring.next()
                            rb = (j // KG) * YCB + YB_G + gi * KG + (j % KG)
                            rb0 = min(rb, 4 * YCB - 2)
                            gsel = rb - rb0
                            S.dma("sp", lambda e, gt=gt, rb0=rb0, pi=pi: e.dma_start(out=gt[:, :, 0:512], in_=yw[:, rb0:rb0 + 2, 2 + pi * 512:2 + (pi + 1) * 512]),
                                  reads=[r_yw], writes=gr)
                            if pi == 0:
                                S.dma("sp", lambda e, gt=gt, rb0=rb0: e.dma_start(out=gt[:, :, 512:514], in_=yw[:, rb0:rb0 + 2, 0:2]),
                                      reads=[r_yw], writes=gr)
                            sg, sgr = tmpr.next()
                            S.op("act", lambda e, sg=sg, gt=gt, gi=gi, j=j, W=W, gsel=gsel: e.activation(out=sg[:, 0:W], in_=gt[:, gsel, 0:W], func=AF.Sigmoid, bias=prmB[:, gi, j:j + 1]),
                                 reads=gr + [r_pb], writes=sgr)
                            gts.append((sg, sgr))
                        act_y = lambda k, c0, c1: yT[:, k, c0:c1]
                        bA, bB, hA, hB = (0, 1, 2, 3) if j % 2 == 0 else (4, 5, 6, 7)
                        gemm(segs, [(wps[j], 0, min(32, nss), ssd_pos[0:min(32, nss)])] +
                             ([(wps[j], 32, nss - 32, ssd_pos[32:nss])] if nss > 32 else []), act_y, [r_yT], bA, hA)
                        gemm(segs, [(wpa[j], 0, KC, att_pos)], act_y, [r_yT], bB, hB)
                        m1, m1r = tmpr.next()
                        for (c0, c1, ish) in segs:
                            pa = ps[hA][:, 0:2] if ish else ps[bA][:]
                            pb = ps[hB][:, 0:2] if ish else ps[bB][:]
                            S.op("dve", lambda e, c0=c0, c1=c1, pa=pa, a=gts[0][0]: e.tensor_tensor(a[:, c0:c1], a[:, c0:c1], pa, ALU.mult), reads=[psr[bA], psr[hA]] + gts[0][1], writes=gts[0][1])
                            S.op("dve", lambda e, c0=c0, c1=c1, pb=pb, b=gts[1][0]: e.tensor_tensor(b[:, c0:c1], b[:, c0:c1], pb, ALU.mult), reads=[psr[bB], psr[hB]] + gts[1][1], writes=gts[1][1])
                        S.op("dve", lambda e, j=j, W=W, a=gts[0][0], b=gts[1][0]: e.tensor_tensor(mrg[:, j, 0:W], a[:, 0:W], b[:, 0:W], ALU.add), reads=gts[0][1] + gts[1][1], writes=[r_mrg])
                    for j in range(KC):
                        b2m, b2h = (0, 2) if j % 2 == 0 else (1, 3)
                        gemm(segs, [(wo[j], 0, KC, list(range(KC)))], lambda k, c0, c1: mrg[:, k, c0:c1], [r_mrg], b2m, b2h)
                        xt_, xr = tmpr.next()
                        S.dma("sp", lambda e, xt_=xt_, j=j, pi=pi: e.dma_start(out=xt_[:, 0:512], in_=xTs[j * 128:(j + 1) * 128, 2 + pi * 512:2 + (pi + 1) * 512]), writes=xr)
                        if pi == 0:
                            S.dma("sp", lambda e, xt_=xt_, j=j: e.dma_start(out=xt_[:, 512:514], in_=xTs[j * 128:(j + 1) * 128, 0:2]), writes=xr)
                        for (c0, c1, ish) in segs:
                            pa = ps[b2h][:, 0:2] if ish else ps[b2m][:]
                            S.op("dve", lambda e, xt_=xt_, c0=c0, c1=c1, pa=pa: e.scalar_tensor_tensor(out=xt_[:, c0:c1], in0=xt_[:, c0:c1], scalar=float(ALPHA), in1=pa, op0=ALU.mult, op1=ALU.add),
                                 reads=[psr[b2m], psr[b2h]] + xr, writes=xr)
                        ln_stats_acc(segs, xt_, xr, j)
                        S.dma("act", lambda e, xt_=xt_, j=j, W=W: e.dma_start(out=hsc[j * 128:(j + 1) * 128, 0:W], in_=xt_[:, 0:W]), reads=xr, writes=[r_hsc])
                    ln_finish(segs)
                    for j in range(KC):
                        ht, hr = tmpr.next()
                        S.dma("sp", lambda e, ht=ht, j=j, W=W: e.dma_start(out=ht[:, 0:W], in_=hsc[j * 128:(j + 1) * 128, 0:W]), reads=[r_hsc], writes=hr)
                        S.op("dve", lambda e, ht=ht, W=W: e.tensor_tensor(ht[:, 0:W], ht[:, 0:W], mean[:, 0:W], ALU.subtract), reads=hr + [r_st], writes=hr)
                        S.op("dve", lambda e, ht=ht, W=W: e.tensor_tensor(ht[:, 0:W], ht[:, 0:W], rstd[:, 0:W], ALU.mult), reads=hr + [r_st], writes=hr)
                        S.op("act", lambda e, ht=ht, W=W, j=j: e.activation(out=ht[:, 0:W], in_=ht[:, 0:W], func=AF.Identity, scale=prmB[:, 2, j:j + 1], bias=prmB[:, 3, j:j + 1]),
                             reads=hr + [r_pb], writes=hr)
                        S.op("dve", lambda e, ht=ht, W=W, j=j: e.tensor_copy(mrg[:, j, 0:W], ht[:, 0:W]), reads=hr, writes=[r_mrg])
                        S.dma("act", lambda e, ht=ht, j=j: e.dma_start(out=h1n[j * 128:(j + 1) * 128, 0:512], in_=ht[:, 0:512]), reads=hr, writes=[r_h1n])
                    for jf in range(NFB):
                        cvs = []
                        for vi in range(2):
                            jj = vi * NFB + jf
                            gemm(segs, [(wup[jj], 0, KC, list(range(KC)))], lambda k, c0, c1: mrg[:, k, c0:c1], [r_mrg], vi, 2 + vi)
                            ue, uer = tmpr.next()
                            S.op("act", lambda e, ue=ue, vi=vi: e.activation(out=ue[:, 2:514], in_=ps[vi][:], func=AF.Identity), reads=[psr[vi]], writes=uer)
                            if pi == 0:
                                S.op("dve", lambda e, ue=ue, vi=vi: e.tensor_scalar_mul(ue[:, 0:2], ps[2 + vi][:, 0:2], hmk[:, 0:1]), reads=[psr[2 + vi], r_pb], writes=uer)
                                S.op("dve", lambda e, ue=ue, jj=jj: e.tensor_copy(usave[:, jj, :], ue[:, 512:514]), reads=uer, writes=[r_us])
                            else:
                                S.op("dve", lambda e, ue=ue, jj=jj: e.tensor_copy(ue[:, 0:2], usave[:, jj, :]), reads=[r_us], writes=uer)
                            cv, cvr = tmpr.next()
                            S.op("act", lambda e, cv=cv, ue=ue, jj=jj: e.activation(out=cv[:, 0:512], in_=ue[:, 2:514], func=AF.Identity, scale=fcwt[:, jj, 2:3], bias=fcbt[:, jj:jj + 1]),
                                 reads=uer + [r_pb], writes=cvr)
                            for k in (1, 0):
                                S.op("dve", lambda e, cv=cv, ue=ue, jj=jj, k=k: e.scalar_tensor_tensor(out=cv[:, 0:512], in0=ue[:, k:k + 512], scalar=fcwt[:, jj, k:k + 1], in1=cv[:, 0:512], op0=ALU.mult, op1=ALU.add),
                                     reads=uer + cvr + [r_pb], writes=cvr)
                            cvs.append((cv, cvr))
                        S.op("act", lambda e, cv=cvs[1][0]: e.activation(out=cv[:, 0:512], in_=cv[:, 0:512], func=AF.Silu), reads=cvs[1][1], writes=cvs[1][1])
                        S.op("dve", lambda e, jf=jf, a=cvs[0][0], b=cvs[1][0]: e.tensor_tensor(yT[:, jf, 0:512], a[:, 0:512], b[:, 0:512], ALU.mult), reads=cvs[0][1] + cvs[1][1], writes=[r_yT])
                    segm = [(0, 512, False)]
                    for j in range(KC):
                        pcs = []
                        k0 = 0
                        while k0 < NFB:
                            kn = min(32, NFB - k0)
                            pcs.append((wdn[j], k0, kn, list(range(k0, k0 + kn))))
                            k0 += kn
                        b4 = j % 2
                        gemm(segm, pcs, lambda k, c0, c1: yT[:, k, c0:c1], [r_yT], b4, 2)
                        ht, hr = tmpr.next()
                        S.dma("sp", lambda e, ht=ht, j=j: e.dma_start(out=ht[:, 0:512], in_=h1n[j * 128:(j + 1) * 128, 0:512]), reads=[r_h1n], writes=hr)
                        S.op("dve", lambda e, ht=ht, b4=b4: e.scalar_tensor_tensor(out=ht[:, 0:512], in0=ht[:, 0:512], scalar=float(ALPHA), in1=ps[b4][:], op0=ALU.mult, op1=ALU.add),
                             reads=[psr[b4]] + hr, writes=hr)
                        ln_stats_acc(segm, ht, hr, j)
                        S.dma("act", lambda e, ht=ht, j=j: e.dma_start(out=hsc[j * 128:(j + 1) * 128, 0:512], in_=ht[:, 0:512]), reads=hr, writes=[r_hsc])
                    ln_finish(segm)
                    for j in range(KC):
                        ht, hr = tmpr.next()
                        S.dma("sp", lambda e, ht=ht, j=j: e.dma_start(out=ht[:, 0:512], in_=hsc[j * 128:(j + 1) * 128, 0:512]), reads=[r_hsc], writes=hr)
                        S.op("dve", lambda e, ht=ht: e.tensor_tensor(ht[:, 0:512], ht[:, 0:512], mean[:, 0:512], ALU.subtract), reads=hr + [r_st], writes=hr)
                        S.op("dve", lambda e, ht=ht: e.tensor_tensor(ht[:, 0:512], ht[:, 0:512], rstd[:, 0:512], ALU.mult), reads=hr + [r_st], writes=hr)
                        S.op("act", lambda e, ht=ht, j=j: e.activation(out=ht[:, 0:512], in_=ht[:, 0:512], func=AF.Identity, scale=prmB[:, 4, j:j + 1], bias=prmB[:, 5, j:j + 1]),
                             reads=hr + [r_pb], writes=hr)
                        S.dma("act", lambda e, ht=ht, j=j, pi=pi: e.dma_start(out=outT[j * 128:(j + 1) * 128, pi * 512:(pi + 1) * 512], in_=ht[:, 0:512]), reads=hr, writes=[r_out])
                S.barrier()
        S.final_wait()
        S.emit()
    return nc, in_names


def blockify(W):
    K, N = W.shape
    return np.ascontiguousarray(W.reshape(K // 128, 128, N // 128, 128).transpose(2, 1, 0, 3))


def pcols(v, nb):
    return np.ascontiguousarray(np.asarray(v, dtype=np.float32).reshape(nb, 128).T)


def bc(v):
    return np.ascontiguousarray(np.broadcast_to(np.asarray(v, dtype=np.float32)[None, :], (128, len(v))))


def prep_inputs(D, inp):
    dm = Dims(D)
    KC, GB, GW, R, FHC, KG = dm.KC, dm.GB, dm.GW, dm.R, dm.FHC, dm.KG
    x = np.asarray(inp["x"], dtype=np.float32)
    w_in = np.asarray(inp["w_in"], dtype=np.float32)[0]
    cw = np.asarray(inp["ssd_conv_w"], dtype=np.float32)[0]
    cbv = np.asarray(inp["ssd_conv_b"], dtype=np.float32)[0]
    shared = {
        "consts": CONST_ARR,
        "gbias": np.ascontiguousarray(np.stack([pcols(inp["gate_bias"][0][0], KC), pcols(inp["gate_bias"][0][1], KC)], axis=1)),
        "wps": blockify(np.asarray(inp["w_proj_ssd"], dtype=np.float32)[0]),
        "wpa": blockify(np.asarray(inp["w_proj_att"], dtype=np.float32)[0]),
        "wo": blockify(np.asarray(inp["w_out"], dtype=np.float32)[0]),
        "wup": blockify(np.asarray(inp["w_up"], dtype=np.float32)[0]),
        "wdn": blockify(np.asarray(inp["w_down"], dtype=np.float32)[0]),
        "lng": np.ascontiguousarray(np.stack([pcols(inp["ln1_g"][0], KC), pcols(inp["ln1_b"][0], KC), pcols(inp["ln2_g"][0], KC), pcols(inp["ln2_b"][0], KC)], axis=1)),
        "fcw": np.ascontiguousarray(np.asarray(inp["ffn_conv_w"], dtype=np.float32)[0].T.reshape(2 * dm.NFB, 128, 3).transpose(1, 0, 2)),
        "fcb": pcols(inp["ffn_conv_b"][0], 2 * dm.NFB),
    }
    xTs = [np.ascontiguousarray(x[b].T) for b in range(NB)]
    per_core = []
    for c in range(8):
        b, hg = c // 4, c % 4
        cols, ccols = [], []
        for gi in range(2):
            g = 2 * hg + gi
            xc_ = list(range(dm.oX + g * GW, dm.oX + (g + 1) * GW))
            bc_ = list(range(dm.oB + g * 128, dm.oB + (g + 1) * 128))
            cc_ = list(range(dm.oC + g * 128, dm.oC + (g + 1) * 128))
            cols += xc_ + bc_ + cc_ + list(range(dm.oZ + g * GW, dm.oZ + (g + 1) * GW))
            ccols += [i - dm.DI for i in xc_ + bc_ + cc_]
        for hh in range(FHC):
            h = hg * FHC + hh
            for o in (dm.oQ, dm.oK, dm.oV):
                cols += list(range(o + h * 128, o + (h + 1) * 128))
        for o in (dm.oG1, dm.oG2):
            cols += list(range(o + hg * KG * 128, o + (hg + 1) * KG * 128))
        heads = [2 * hg * R + i for i in range(2 * R)]
        scols = [dm.oDT + h for h in heads] + [dm.oF + hg * FHC + hh for hh in range(FHC)]
        chans = []
        for gi in range(2):
            g = 2 * hg + gi
            chans += list(range(g * GW, (g + 1) * GW))
        t0 = hg * 1024
        xs = np.zeros((D, 1026), dtype=np.float32)
        xs[:, 2:] = xTs[b][:, t0:t0 + 1024]
        if t0 > 0:
            xs[:, 0:2] = xTs[b][:, t0 - 2:t0]
        m = dict(shared)
        m.update({
            "xT": xTs[b],
            "wA": blockify(w_in[:, cols]),
            "wS": np.ascontiguousarray(w_in[:, scols].reshape(KC, 128, len(scols)).transpose(1, 0, 2)),
            "convw": np.ascontiguousarray(cw[:, ccols].T.reshape(2 * (GB + 2), 128, 4).transpose(1, 0, 2)),
            "convb": pcols(cbv[ccols], 2 * (GB + 2)),
            "dtb": bc(np.asarray(inp["ssd_dt_bias"], dtype=np.float32)[0][heads]),
            "alog": bc(np.asarray(inp["ssd_a_log"], dtype=np.float32)[0][heads]),
            "dskip": bc(np.repeat(np.asarray(inp["ssd_d"], dtype=np.float32)[0][heads], 64)),
            "normw": bc(np.asarray(inp["ssd_norm_w"], dtype=np.float32)[0][chans]),
            "fbias": bc(np.asarray(inp["fox_f_bias"], dtype=np.float32)[0][hg * FHC:(hg + 1) * FHC]),
            "xTs": xs,
            "t0": np.array([[t0, max(t0 - 2, 0), t0 + 512]], dtype=np.int32),
            "hmask": np.full((128, 1), 0.0 if hg == 0 else 1.0, dtype=np.float32),
        })
        per_core.append(m)
    return per_core


_NC_CACHE = {}


def run(D, inp, stages=("A", "X", "B"), extra=None, debug=False):
    key = (D, tuple(stages), debug)
    if key not in _NC_CACHE:
        _NC_CACHE[key] = build_program(D, stages, debug)
    nc, decl = _NC_CACHE[key]
    per_core = prep_inputs(D, inp)
    maps = []
    for c in range(8):
        m = per_core[c]
        if extra is not None:
            m.update(extra[c])
        maps.append({k: v for k, v in m.items() if k in decl})
    res = run_bass_kernel_spmd(nc, maps, core_ids=list(range(8)))
    return res.results


def kernel(**inputs):
    D = 4096
    res = run(D, inputs)
    out = np.zeros((NB, L, D), dtype=np.float32)
    for c in range(8):
        b, tq = c // 4, c % 4
        out[b, tq * 1024:(tq + 1) * 1024, :] = np.asarray(res[c]["outT"]).T
    return out
```

```python
import contextlib
import numpy as np
import ml_dtypes
import concourse.bass as bass
import concourse.mybir as mybir
from concourse.bass_utils import run_bass_kernel_spmd

F32 = mybir.dt.float32
BF16 = mybir.dt.bfloat16
I32 = mybir.dt.int32
AF = mybir.ActivationFunctionType
ALU = mybir.AluOpType
AX = mybir.AxisListType

L = 4096
NB = 2
NBLKT = L // 128
ALPHA = 2.0 ** 0.25
LN_EPS = 1e-5
RMS_EPS = 1e-5


class Dims:
    def __init__(self, D):
        self.D = D
        self.KC = D // 128
        self.DI = 2 * D
        self.SH = self.DI // 64
        self.R = self.SH // 8
        self.GW = self.R * 64
        self.GB = self.GW // 128
        self.FH = D // 128
        self.FHC = self.FH // 4
        self.DFF = ((8 * D // 3 + 255) // 256) * 256
        self.NFB = self.DFF // 128
        self.NS = 2 * self.R + self.FHC
        self.KG = self.KC // 4
        self.NBLK = 2 * (2 * self.GB + 2) + 3 * self.FHC + 2 * self.KG
        self.YCB = 2 * self.GB + self.FHC + 2 * self.KG
        self.oZ = 0
        self.oX = self.DI
        self.oB = 2 * self.DI
        self.oC = 2 * self.DI + 1024
        self.oDT = 2 * self.DI + 2048
        self.oQ = self.oDT + self.SH
        self.oK = self.oQ + D
        self.oV = self.oK + D
        self.oF = self.oV + D
        self.oG1 = self.oF + self.FH
        self.oG2 = self.oG1 + D


class Res:
    __slots__ = ("w", "r")

    def __init__(self):
        self.w = None
        self.r = {}


class Sched:
    def __init__(self, nc, stack, ndma=20):
        self.nc = nc
        self.engs = {"pe": nc.tensor, "act": nc.scalar, "dve": nc.vector, "pool": nc.gpsimd, "sp": nc.sync}
        self.prog = {k: [] for k in self.engs}
        self.sems = {}
        for k in self.engs:
            self.sems[k] = stack.enter_context(nc.semaphore("s_" + k))
        self.cnt = {k: 0 for k in self.engs}
        self.ndma = ndma
        self.qring = {"sp": list(range(0, 8)), "act": list(range(8, 16)), "pool": list(range(16, 20))}
        self.qnext = {"sp": 0, "act": 0, "pool": 0}
        for i in range(ndma):
            self.sems[("d", i)] = stack.enter_context(nc.semaphore("d%d" % i))
        self.sems["cc"] = stack.enter_context(nc.semaphore("cc"))
        self.ccval = 0
        self.dval = [0] * ndma
        self.dnobar = [False] * ndma
        self.dnext = 0
        self.seen = {k: {} for k in self.engs}

    def _wait(self, eng, ev):
        if ev is None:
            return
        key, val = ev
        if eng == "pe" and key == "pe":
            return
        if self.seen[eng].get(key, 0) >= val:
            return
        self.seen[eng][key] = val
        s = self.sems[key]
        self.prog[eng].append(lambda e, s=s, v=val: e.wait_ge(s, v))

    def _deps(self, eng, reads, writes):
        for r in reads:
            self._wait(eng, r.w)
        for w in writes:
            self._wait(eng, w.w)
            for k, v in w.r.items():
                self._wait(eng, (k, v))

    def _commit(self, ev, reads, writes):
        for r in reads:
            if r.r.get(ev[0], 0) < ev[1]:
                r.r[ev[0]] = ev[1]
        for w in writes:
            w.w = ev
            w.r = {}

    def op(self, eng, fns, reads=(), writes=()):
        if not isinstance(fns, (list, tuple)):
            fns = [fns]
        self._deps(eng, reads, writes)
        self.cnt[eng] += 1
        ev = (eng, self.cnt[eng])
        for f in fns[:-1]:
            self.prog[eng].append(f)
        s = self.sems[eng]
        self.prog[eng].append(lambda e, f=fns[-1], s=s: f(e).then_inc(s, 1))
        self._commit(ev, reads, writes)

    def dma(self, q, fn, reads=(), writes=(), nobar=False):
        ring = self.qring[q]
        i = ring[self.qnext[q] % len(ring)]
        self.qnext[q] += 1
        key = ("d", i)
        if self.dval[i] > 0:
            self._wait(q, (key, self.dval[i]))
        self._deps(q, reads, writes)
        self.dval[i] += 16
        self.dnobar[i] = nobar
        ev = (key, self.dval[i])
        s = self.sems[key]
        self.prog[q].append(lambda e, f=fn, s=s: f(e).then_inc(s, 16))
        self._commit(ev, reads, writes)

    def coll(self, fn, reads=(), writes=()):
        self._deps("pool", reads, writes)
        self.ccval += 1
        ev = ("cc", self.ccval)
        s = self.sems["cc"]
        self.prog["pool"].append(lambda e, f=fn, s=s: f(e).then_inc(s))
        self._commit(ev, reads, writes)

    def barrier(self, with_cc=False):
        for eng in self.engs:
            for k in self.engs:
                if k != eng and self.cnt[k] > 0:
                    self._wait(eng, (k, self.cnt[k]))
            for i in range(self.ndma):
                if self.dval[i] > 0 and not (self.dnobar[i] and not with_cc):
                    self._wait(eng, (("d", i), self.dval[i]))
            if with_cc and self.ccval > 0:
                self._wait(eng, ("cc", self.ccval))

    def final_wait(self):
        for i in range(self.ndma):
            if self.dval[i] > 0:
                self._wait("sp", (("d", i), self.dval[i]))
        for k in self.engs:
            if k != "sp" and self.cnt[k] > 0:
                self._wait("sp", (k, self.cnt[k]))

    def emit(self):
        nc = self.nc
        prog = self.prog
        with nc.Block() as block:
            @block.tensor
            def _(e):
                for f in prog["pe"]:
                    f(e)

            @block.scalar
            def _(e):
                for f in prog["act"]:
                    f(e)

            @block.vector
            def _(e):
                for f in prog["dve"]:
                    f(e)

            @block.gpsimd
            def _(e):
                for f in prog["pool"]:
                    f(e)

            @block.sync
            def _(e):
                for f in prog["sp"]:
                    f(e)


class Ring:
    def __init__(self, nc, stack, name, n, shape, dtype, nres=1):
        self.t = [stack.enter_context(nc.sbuf_tensor("%s%d" % (name, i), shape, dtype)) for i in range(n)]
        self.res = [[Res() for _ in range(nres)] for _ in range(n)]
        self.i = 0
        self.n = n

    def next(self):
        i = self.i
        self.i = (i + 1) % self.n
        return self.t[i], self.res[i]


def mm(out, lhsT, rhs, start, stop):
    return lambda e: e.matmul(out, lhsT, rhs, start=start, stop=stop)


def build_consts():
    i = np.arange(128)
    same = (i[:, None] // 64) == (i[None, :] // 64)
    c = {}
    c["ident"] = np.eye(128)
    c["MU"] = ((i[:, None] <= i[None, :]) & same)
    c["ML"] = ((i[:, None] > i[None, :]) & same)
    c["HO0"] = np.broadcast_to((i[:, None] < 64), (128, 128))
    c["HO1"] = np.broadcast_to((i[:, None] >= 64), (128, 128))
    c["CM"] = (i[:, None] <= i[None, :])
    c["ONES"] = np.ones((128, 128))
    c["E0"] = np.broadcast_to((i[:, None] == 0), (128, 128))
    c["COL0"] = np.broadcast_to((i[None, :] < 64), (128, 128))
    c["COL1"] = np.broadcast_to((i[None, :] >= 64), (128, 128))
    names = list(c.keys())
    arr = np.stack([np.asarray(c[n], dtype=np.float32) for n in names], axis=1)
    return names, np.ascontiguousarray(arr)


CONST_NAMES, CONST_ARR = build_consts()
NCONST = len(CONST_NAMES)


def build_program(D, stages=("A", "X", "B"), debug=False):
    dm = Dims(D)
    KC, GB, GW, R, FHC, NS, KG, NBLK, YCB, NFB = dm.KC, dm.GB, dm.GW, dm.R, dm.FHC, dm.NS, dm.KG, dm.NBLK, dm.YCB, dm.NFB
    nc = bass.Bass("TRN2", target_bir_lowering=False)
    doA, doX, doB = "A" in stages, "X" in stages, "B" in stages

    in_names = []

    def din(name, shape, dt=F32):
        in_names.append(name)
        return nc.dram_tensor(name, list(shape), dt, kind="ExternalInput").ap()

    def dout(name, shape, dt=F32):
        return nc.dram_tensor(name, list(shape), dt, kind="ExternalOutput").ap()

    def dscr(name, shape, dt, ext=None):
        if ext == "in":
            return din(name, shape, dt)
        if ext == "out":
            return dout(name, shape, dt)
        return nc.dram_tensor(name, list(shape), dt).ap()

    YROWS = YCB * 128
    YB_G, YB_S, YB_A = 0, 2 * KG, 2 * KG + 2 * GB
    LP = L + 2
    consts_d = din("consts", [128, NCONST, 128])
    t0d = din("t0", [1, 3], I32) if doX else None
    if doA:
        xT = din("xT", [D, L])
        wA = din("wA", [NBLK, 128, KC, 128])
        wS = din("wS", [128, KC, NS])
        convw = din("convw", [128, 2 * (GB + 2), 4])
        convb = din("convb", [128, 2 * (GB + 2)])
        dtb = din("dtb", [128, 2 * R])
        alog = din("alog", [128, 2 * R])
        dskip = din("dskip", [128, 2 * GW])
        normw = din("normw", [128, 2 * GW])
        fbias = din("fbias", [128, FHC])
        xTb = dscr("xTb", [D, L], BF16)
        projT = dscr("projT", [NBLK * 128, L], BF16, "out" if (debug and not doB) else None)
    ycat = dscr("ycat", [YROWS, L], BF16, ("out" if not doX else None) if doA else None) if doA else None
    yall = dscr("yall", [4 * YROWS, L], BF16) if doX else None
    yall_dbg = dout("yall_dbg", [4 * YROWS, L], BF16) if (doX and not doB) else None
    ywin = dscr("ywin", [4 * YROWS, 1026], BF16, None if doX else "in") if doB or doX else None
    if doB:
        xTs = din("xTs", [D, 1026])
        hmask = din("hmask", [128, 1])
        gbias = din("gbias", [128, 2, KC])
        wps = din("wps", [KC, 128, 2 * KC, 128])
        wpa = din("wpa", [KC, 128, KC, 128])
        wo = din("wo", [KC, 128, KC, 128])
        wup = din("wup", [2 * NFB, 128, KC, 128])
        wdn = din("wdn", [KC, 128, NFB, 128])
        lng = din("lng", [128, 4, KC])
        fcw = din("fcw", [128, 2 * NFB, 3])
        fcb = din("fcb", [128, 2 * NFB])
        hsc = dscr("hsc", [D, 514], F32)
        h1n = dscr("h1n", [D, 514], F32)
        outT = dout("outT", [D, 1024])

    with contextlib.ExitStack() as top:
        S = Sched(nc, top)
        ps = [top.enter_context(nc.psum_tensor("ps%d" % i, [128, 512], F32)) for i in range(8)]
        psr = [Res() for _ in range(8)]
        cf = top.enter_context(nc.sbuf_tensor("cf", [128, NCONST, 128], F32))
        cb = top.enter_context(nc.sbuf_tensor("cb", [128, NCONST, 128], BF16))
        r_c = Res()
        S.dma("sp", lambda e: e.dma_start(out=cf[:], in_=consts_d[:, :, :]), writes=[r_c])
        S.op("dve", lambda e: e.tensor_copy(cb[:], cf[:]), reads=[r_c], writes=[r_c])

        def CF(n):
            return cf[:, CONST_NAMES.index(n), :]

        def CB(n):
            return cb[:, CONST_NAMES.index(n), :]

        r_xTb, r_proj, r_yall = Res(), Res(), Res()
        r_ycb = [Res() for _ in range(YCB)]
        cstate = {'n': 0}
        NCHK = YCB

        dyn = {}
        r_yw = Res()

        def dv(e, i, q="pool"):
            if (q, i) not in dyn:
                r = top.enter_context(e.register("t0r%s%d" % (q, i)))
                e.reg_load(r, t0d[0:1, i:i + 1])
                dyn[(q, i)] = e.snap(r, min_val=0, max_val=(3072, 3070, 3584)[i])
            return dyn[(q, i)]

        def emit_wcopy(jb0, jb1):
            ya4 = yall.rearrange("(k r p) t -> k r p t", r=4, p=128)
            for r_ in range(4):
                S.dma("pool", lambda e, r_=r_: e.dma_start(
                    out=ywin[r_ * YROWS + jb0 * 128:r_ * YROWS + jb1 * 128, 2:1026].rearrange("(k p) t -> k p t", p=128),
                    in_=ya4[jb0:jb1, r_, :, bass.ds(dv(e, 0), 1024)]), reads=[r_yall], writes=[r_yw], nobar=True)
                S.dma("pool", lambda e, r_=r_: e.dma_start(
                    out=ywin[r_ * YROWS + jb0 * 128:r_ * YROWS + jb1 * 128, 0:2].rearrange("(k p) t -> k p t", p=128),
                    in_=ya4[jb0:jb1, r_, :, bass.ds(dv(e, 1), 2)]), reads=[r_yall], writes=[r_yw], nobar=True)

        def emit_coll(kmax):
            while cstate['n'] < kmax:
                k = cstate['n']
                S.coll(lambda e, k=k: e.collective_compute('AllGather', ALU.bypass, [[0, 1, 2, 3], [4, 5, 6, 7]], ins=[ycat[k * 128:(k + 1) * 128, :].opt()], outs=[yall[k * 512:(k + 1) * 512, :].opt()]),
                       reads=[r_ycb[k]], writes=[r_yall])
                cstate['n'] += 1


        if doA:
            with contextlib.ExitStack() as st:
                for kc in range(KC):
                    S.dma("pool", lambda e, kc=kc: e.dma_start(out=xTb[kc * 128:(kc + 1) * 128, :],
                                                                in_=xT[kc * 128:(kc + 1) * 128, :]), writes=[r_xTb])
                S.barrier()

            stA = contextlib.ExitStack()
            raws = stA.enter_context(nc.sbuf_tensor("raws", [128, NBLKT, NS], F32))
            r_raws = Res()

            gate0 = NBLK - 2 * KG
            with contextlib.ExitStack() as st:
                wring = Ring(nc, st, "wA", 2, [128, 4, KC, 128], BF16)
                xring = Ring(nc, st, "xc", 2, [128, KC, 512], BF16)
                sring = Ring(nc, st, "stg", 3, [128, 512], BF16)
                wsm = st.enter_context(nc.sbuf_tensor("wsm", [128, KC, NS], BF16))
                r_wsm = Res()
                S.dma("pool", lambda e: e.dma_start(out=wsm[:], in_=wS[:, :, :]), writes=[r_wsm])
                units = [list(range(u, min(u + 4, NBLK))) for u in range(0, NBLK, 4)]
                nev = 0
                for ui, unit in enumerate(units):
                    wt, wr = wring.next()
                    for bi, j in enumerate(unit):
                        S.dma("pool", lambda e, wt=wt, bi=bi, j=j: e.dma_start(out=wt[:, bi], in_=wA[j]), writes=wr)
                    for tc in range(8):
                        xt_, xr = xring.next()
                        S.dma("sp", lambda e, xt_=xt_, tc=tc: e.dma_start(
                            out=xt_[:], in_=xTb.rearrange("(k p) t -> p k t", p=128)[:, :, tc * 512:(tc + 1) * 512]),
                            reads=[r_xTb], writes=xr)
                        for bi, j in enumerate(unit):
                            bk = nev % 2
                            S.op("pe", [mm(ps[bk][:], wt[:, bi, kc, :], xt_[:, kc, :], kc == 0, kc == KC - 1)
                                        for kc in range(KC)], reads=wr + xr, writes=[psr[bk]])
                            sg, sr = sring.next()
                            if nev % 2 == 0:
                                S.op("act", lambda e, sg=sg, bk=bk: e.activation(out=sg[:], in_=ps[bk][:], func=AF.Identity),
                                     reads=[psr[bk]], writes=sr)
                            else:
                                S.op("dve", lambda e, sg=sg, bk=bk: e.tensor_copy(sg[:], ps[bk][:]),
                                     reads=[psr[bk]], writes=sr)
                            nev += 1
                            if j < gate0:
                                S.dma("act", lambda e, sg=sg, j=j, tc=tc: e.dma_start(
                                    out=projT[j * 128:(j + 1) * 128, tc * 512:(tc + 1) * 512], in_=sg[:]),
                                    reads=sr, writes=[r_proj])
                            else:
                                yb = YB_G + (j - gate0)
                                S.dma("act", lambda e, sg=sg, yb=yb, tc=tc: e.dma_start(
                                    out=ycat[yb * 128:(yb + 1) * 128, tc * 512:(tc + 1) * 512], in_=sg[:]),
                                    reads=sr, writes=[r_ycb[yb]])
                        if ui == 0:
                            for tb in range(4):
                                blk = tc * 4 + tb
                                S.op("pe", [mm(ps[2][:, 0:NS], xt_[:, kc, tb * 128:(tb + 1) * 128], wsm[:, kc, :], kc == 0, kc == KC - 1)
                                            for kc in range(KC)], reads=xr + [r_wsm], writes=[psr[2]])
                                S.op("dve", lambda e, blk=blk: e.tensor_copy(raws[:, blk, :], ps[2][:, 0:NS]),
                                     reads=[psr[2]], writes=[r_raws])
                S.barrier()
            if doX:
                emit_coll(YB_S)
                emit_wcopy(0, YB_S)

            NH = 2 * R
            NSC = NBLKT * NH
            dt_ = stA.enter_context(nc.sbuf_tensor("dt_", [128, NBLKT, NH], F32))
            da_ = stA.enter_context(nc.sbuf_tensor("da_", [128, NBLKT, NH], F32))
            eacs = stA.enter_context(nc.sbuf_tensor("eacs", [128, NBLKT, NH], F32))
            wend = stA.enter_context(nc.sbuf_tensor("wend", [128, NBLKT, NH], F32))
            cdec = [stA.enter_context(nc.sbuf_tensor("cdec%d" % h, [128, NBLKT, NH], F32)) for h in range(2)]
            Ftok = stA.enter_context(nc.sbuf_tensor("Ftok", [128, NBLKT, FHC], F32))
            Fq = stA.enter_context(nc.sbuf_tensor("Fq", [128, NBLKT, FHC], F32))
            r_sm = Res()
            with contextlib.ExitStack() as st:
                prm = st.enter_context(nc.sbuf_tensor("prm", [128, 4 * R + FHC], F32))
                r_prm = Res()
                S.dma("sp", lambda e: e.dma_start(out=prm[:, 0:NH], in_=dtb[:, :]), writes=[r_prm])
                S.dma("sp", lambda e: e.dma_start(out=prm[:, NH:2 * NH], in_=alog[:, :]), writes=[r_prm])
                S.dma("sp", lambda e: e.dma_start(out=prm[:, 2 * NH:2 * NH + FHC], in_=fbias[:, :]), writes=[r_prm])
                tmp = st.enter_context(nc.sbuf_tensor("smtmp", [128, NBLKT, NH], F32))
                lf = st.enter_context(nc.sbuf_tensor("lf", [128, NBLKT, FHC], F32))
                pref = st.enter_context(nc.sbuf_tensor("pref", [128, NBLKT, FHC], F32))
                an = st.enter_context(nc.sbuf_tensor("an", [128, NH], F32))
                rt = Res()
                S.op("act", lambda e: e.activation(out=an[:], in_=prm[:, NH:2 * NH], func=AF.Exp), reads=[r_prm], writes=[rt])
                S.op("dve", lambda e: e.tensor_scalar_mul(an[:], an[:], -1.0), reads=[rt], writes=[rt])
                S.op("dve", lambda e: e.tensor_tensor(tmp[:], raws[:, :, 0:NH], prm[:, 0:NH].unsqueeze(1).to_broadcast([128, NBLKT, NH]), ALU.add),
                     reads=[r_raws, r_prm], writes=[rt])
                S.op("act", lambda e: e.activation(out=tmp[:], in_=tmp[:], func=AF.Exp), reads=[rt], writes=[rt])
                S.op("act", lambda e: e.activation(out=dt_[:], in_=tmp[:], func=AF.Ln, bias=1.0), reads=[rt], writes=[r_sm])
                S.op("dve", lambda e: e.tensor_tensor(da_[:], dt_[:], an[:].unsqueeze(1).to_broadcast([128, NBLKT, NH]), ALU.mult),
                     reads=[r_sm, rt], writes=[r_sm])
                daf = da_[:].rearrange("p b h -> p (b h)")

                def cum(dst, lhs_name, bank):
                    for c0 in range(0, NSC, 512):
                        c1 = min(NSC, c0 + 512)
                        S.op("pe", mm(ps[bank][:, 0:c1 - c0], CF(lhs_name), daf[:, c0:c1], True, True), reads=[r_sm, r_c], writes=[psr[bank]])
                        S.op("act", lambda e, c0=c0, c1=c1: e.activation(out=dst[:].rearrange("p b h -> p (b h)")[:, c0:c1], in_=ps[bank][:, 0:c1 - c0], func=AF.Exp),
                             reads=[psr[bank]], writes=[r_sm])
                cum(eacs, "MU", 0)
                cum(wend, "ML", 1)
                cum(cdec[0], "HO0", 2)
                cum(cdec[1], "HO1", 3)
                S.op("dve", lambda e: e.tensor_tensor(lf[:], raws[:, :, NH:NH + FHC], prm[:, 2 * NH:2 * NH + FHC].unsqueeze(1).to_broadcast([128, NBLKT, FHC]), ALU.add),
                     reads=[r_raws, r_prm], writes=[rt])
                S.op("act", lambda e: e.activation(out=lf[:], in_=lf[:], func=AF.Exp, scale=-1.0), reads=[rt], writes=[rt])
                S.op("act", lambda e: e.activation(out=lf[:], in_=lf[:], func=AF.Ln, bias=1.0), reads=[rt], writes=[rt])
                S.op("dve", lambda e: e.tensor_scalar_mul(lf[:], lf[:], -1.0), reads=[rt], writes=[rt])
                S.op("dve", lambda e: e.memset(pref[:, 0, :], 0.0), writes=[rt])
                for b_ in range(1, NBLKT):
                    S.op("dve", lambda e, b_=b_: e.tensor_tensor(pref[:, b_, :], pref[:, b_ - 1, :], lf[:, b_ - 1, :], ALU.add), reads=[rt], writes=[rt])
                nf = NBLKT * FHC
                S.op("pe", [mm(ps[4][:, 0:nf], CF("CM"), lf[:].rearrange("p b h -> p (b h)"), True, False),
                            mm(ps[4][:, 0:nf], CF("ONES"), pref[:].rearrange("p b h -> p (b h)"), False, True)],
                     reads=[rt, r_c], writes=[psr[4]])
                S.op("dve", lambda e: e.tensor_copy(Ftok[:].rearrange("p b h -> p (b h)"), ps[4][:, 0:nf]), reads=[psr[4]], writes=[r_sm])
                S.op("pe", mm(ps[5][:, 0:nf], CF("E0"), Ftok[:].rearrange("p b h -> p (b h)"), True, True), reads=[r_sm, r_c], writes=[psr[5]])
                S.op("dve", lambda e: e.tensor_copy(Fq[:].rearrange("p b h -> p (b h)"), ps[5][:, 0:nf]), reads=[psr[5]], writes=[r_sm])
                S.barrier()

            NCH = min(512, GW)
            NHH = GW // NCH
            HPH = NCH // 64
            with contextlib.ExitStack() as st:
                cwt = st.enter_context(nc.sbuf_tensor("cwt", [128, 2 * (GB + 2), 4], F32))
                cbt = st.enter_context(nc.sbuf_tensor("cbt", [128, 2 * (GB + 2)], F32))
                dsk = st.enter_context(nc.sbuf_tensor("dsk", [128, 2 * GW], F32))
                nrw = st.enter_context(nc.sbuf_tensor("nrw", [128, 2 * GW], F32))
                r_p = Res()
                S.dma("sp", lambda e: e.dma_start(out=cwt[:], in_=convw[:, :, :]), writes=[r_p])
                S.dma("sp", lambda e: e.dma_start(out=cbt[:], in_=convb[:, :]), writes=[r_p])
                S.dma("sp", lambda e: e.dma_start(out=dsk[:], in_=dskip[:, :]), writes=[r_p])
                S.dma("sp", lambda e: e.dma_start(out=nrw[:], in_=normw[:, :]), writes=[r_p])
                rawr = Ring(nc, st, "raw", 2, [128, GB + 2, 515], BF16)
                zr_ = Ring(nc, st, "zraw", 1, [128, GB, 512], BF16)
                accr = Ring(nc, st, "acc", 2, [128, 512], F32)
                xsr = Ring(nc, st, "xs", 1, [128, GB + 2, 512], BF16)
                szr = Ring(nc, st, "sz", 1, [128, GB, 512], BF16)
                H = [st.enter_context(nc.sbuf_tensor("H%d" % g, [128, GW], F32)) for g in range(2)]
                Hb = [[st.enter_context(nc.sbuf_tensor("Hb%d_%d" % (g, i), [128, GW], BF16)) for i in range(2)] for g in range(2)]
                r_H = [Res(), Res()]
                r_Hb = [[Res(), Res()], [Res(), Res()]]
                btk = Ring(nc, st, "btk", 2, [128, 128], BF16)
                xdtr = Ring(nc, st, "xdt", 2, [128, GW], BF16)
                ur = Ring(nc, st, "u", 2, [128, GW], BF16)
                xDr = Ring(nc, st, "xD", 2, [128, GW], F32)
                cbm = Ring(nc, st, "cbm", 2, [128, 128], F32)
                c01 = Ring(nc, st, "c01", 2, [128, 2, 128], BF16)
                lhr = Ring(nc, st, "lh", 2, [128, R, 128], F32)
                er = Ring(nc, st, "E", 2, [128, 512], F32)
                mtr = Ring(nc, st, "MT", 2, [128, 4, 128], BF16)
                t1r = Ring(nc, st, "t1", 2, [128, NCH], F32)
                yzr = Ring(nc, st, "yz", 2, [128, GW], F32)
                sqr = Ring(nc, st, "sq", 1, [128, GW], F32)
                ssr = Ring(nc, st, "ss", 2, [128, 2], F32)
                ynr = Ring(nc, st, "yn", 2, [128, GW], BF16)
                ysr = Ring(nc, st, "ysg", 1, [128, GB, 512], BF16)
                for g in range(2):
                    S.op("dve", lambda e, g=g: e.memset(H[g][:], 0.0), writes=[r_H[g]])
                    S.op("dve", lambda e, g=g: e.memset(Hb[g][0][:], 0.0), writes=[r_Hb[g][0]])
                hbi = [0, 0]
                for sc in range(8):
                    for g in range(2):
                        jx = g * (2 * GB + 2)
                        raw, rr_ = rawr.next()
                        if sc == 0:
                            S.op("dve", lambda e, raw=raw: e.memset(raw[:, :, 0:3], 0.0), writes=rr_)
                            S.dma("sp", lambda e, raw=raw, jx=jx: e.dma_start(out=raw[:, :, 3:515], in_=projT[jx * 128:(jx + GB + 2) * 128, 0:512].rearrange("(j p) t -> p j t", p=128)),
                                  reads=[r_proj], writes=rr_)
                        else:
                            S.dma("sp", lambda e, raw=raw, jx=jx, sc=sc: e.dma_start(out=raw[:], in_=projT[jx * 128:(jx + GB + 2) * 128, sc * 512 - 3:sc * 512 + 512].rearrange("(j p) t -> p j t", p=128)),
                                  reads=[r_proj], writes=rr_)
                        zraw, zrr = zr_.next()
                        jz = jx + GB + 2
                        S.dma("sp", lambda e, zraw=zraw, jz=jz, sc=sc: e.dma_start(out=zraw[:], in_=projT[jz * 128:(jz + GB) * 128, sc * 512:(sc + 1) * 512].rearrange("(j p) t -> p j t", p=128)),
                              reads=[r_proj], writes=zrr)
                        xs, xsres = xsr.next()
                        sz, szres = szr.next()
                        for b_ in range(GB + 2):
                            acc, ar = accr.next()
                            ci = g * (GB + 2) + b_
                            S.op("act", lambda e, acc=acc, raw=raw, b_=b_, ci=ci: e.activation(out=acc[:], in_=raw[:, b_, 3:515], func=AF.Identity,
                                                                                        bias=cbt[:, ci:ci + 1], scale=cwt[:, ci, 3:4]), reads=rr_ + [r_p], writes=ar)
                            for k in (2, 1, 0):
                                S.op("dve", lambda e, acc=acc, raw=raw, b_=b_, ci=ci, k=k: e.scalar_tensor_tensor(
                                    out=acc[:], in0=raw[:, b_, k:k + 512], scalar=cwt[:, ci, k:k + 1], in1=acc[:], op0=ALU.mult, op1=ALU.add),
                                    reads=rr_ + [r_p] + ar, writes=ar)
                            S.op("act", lambda e, acc=acc, xs=xs, b_=b_: e.activation(out=xs[:, b_, :], in_=acc[:], func=AF.Silu), reads=ar, writes=xsres)
                        S.op("act", lambda e, sz=sz, zraw=zraw: e.activation(out=sz[:], in_=zraw[:], func=AF.Silu), reads=zrr, writes=szres)
                        ysg, ysres = ysr.next()
                        def front(tb, g=g, sc=sc, xs=xs, xsres=xsres):
                            blk = sc * 4 + tb
                            tsl = slice(tb * 128, (tb + 1) * 128)
                            hsl = slice(g * R, (g + 1) * R)
                            pxb = ps[0][:].bitcast(BF16)
                            S.op("pe", [lambda e, b_=b_, xs=xs, tsl=tsl, pxb=pxb: e.transpose(pxb[:, b_ * 128:(b_ + 1) * 128], xs[:, b_, tsl], CB("ident")) for b_ in range(GB)],
                                 reads=xsres + [r_c], writes=[psr[0]])
                            pbb = ps[1][:].bitcast(BF16)
                            S.op("pe", lambda e, xs=xs, tsl=tsl, pbb=pbb: e.transpose(pbb[:, 0:128], xs[:, GB, tsl], CB("ident")), reads=xsres + [r_c], writes=[psr[1]])
                            bt, btr = btk.next()
                            S.op("dve", lambda e, bt=bt, pbb=pbb: e.tensor_copy(bt[:], pbb[:, 0:128]), reads=[psr[1]], writes=btr)
                            xdt, xdr = xdtr.next()
                            u_, u_r = ur.next()
                            xD, xDres = xDr.next()
                            px3 = pxb[:, 0:GW].rearrange("p (r c) -> p r c", c=64)
                            S.op("dve", lambda e, xdt=xdt, px3=px3, blk=blk, hsl=hsl: e.tensor_tensor(xdt[:].rearrange("p (r c) -> p r c", c=64), px3,
                                                                                               dt_[:, blk, hsl].unsqueeze(2).to_broadcast([128, R, 64]), ALU.mult),
                                 reads=[psr[0], r_sm], writes=xdr)
                            S.op("dve", lambda e, xdt=xdt, u_=u_, blk=blk, hsl=hsl: e.tensor_tensor(u_[:].rearrange("p (r c) -> p r c", c=64), xdt[:].rearrange("p (r c) -> p r c", c=64),
                                                                                              wend[:, blk, hsl].unsqueeze(2).to_broadcast([128, R, 64]), ALU.mult),
                                 reads=xdr + [r_sm], writes=u_r)
                            S.op("dve", lambda e, xD=xD, pxb=pxb, g=g: e.tensor_tensor(xD[:], pxb[:, 0:GW], dsk[:, g * GW:(g + 1) * GW], ALU.mult),
                                 reads=[psr[0], r_p], writes=xDres)
                            S.op("pe", mm(ps[1][:, 128:256], xs[:, GB, tsl], xs[:, GB + 1, tsl], True, True), reads=xsres + [psr[1]], writes=[psr[1]])
                            cm_, cmr = cbm.next()
                            S.op("dve", lambda e, cm_=cm_: e.tensor_tensor(cm_[:], ps[1][:, 128:256], CF("MU"), ALU.mult), reads=[psr[1], r_c], writes=cmr)
                            cc_, ccr = c01.next()
                            S.op("dve", lambda e, cc_=cc_, xs=xs, tsl=tsl: e.tensor_tensor(cc_[:, 0, :], xs[:, GB + 1, tsl], CB("COL0"), ALU.mult), reads=xsres + [r_c], writes=ccr)
                            S.op("dve", lambda e, cc_=cc_, xs=xs, tsl=tsl: e.tensor_tensor(cc_[:, 1, :], xs[:, GB + 1, tsl], CB("COL1"), ALU.mult), reads=xsres + [r_c] + ccr, writes=ccr)
                            lh, lhres = lhr.next()
                            S.op("act", [lambda e, lh=lh, blk=blk, g=g, r_=r_: e.activation(out=lh[:, r_, :], in_=CF("ML"), func=AF.Copy,
                                                                                      scale=da_[:, blk, g * R + r_:g * R + r_ + 1]) for r_ in range(R)],
                                 reads=[r_c, r_sm], writes=lhres)
                            return (bt, btr, xdt, xdr, u_, u_r, xD, xDres, cm_, cmr, cc_, ccr, lh, lhres)

                        def back(tb, FR, g=g, sc=sc, xs=xs, xsres=xsres, sz=sz, szres=szres, ysg=ysg, ysres=ysres):
                            (bt, btr, xdt, xdr, u_, u_r, xD, xDres, cm_, cmr, cc_, ccr, lh, lhres) = FR
                            blk = sc * 4 + tb
                            tsl = slice(tb * 128, (tb + 1) * 128)
                            hsl = slice(g * R, (g + 1) * R)
                            yz, yzres = yzr.next()
                            for hp in range(2):
                                cur = hbi[g]
                                nxt = 1 - cur
                                for hf in range(NHH):
                                    csl = slice(hf * NCH, (hf + 1) * NCH)
                                    S.op("pe", mm(ps[4 + hf][:, 0:NCH], cc_[:, hp, :], Hb[g][cur][:, csl], hp == 0, hp == 1),
                                         reads=ccr + [r_Hb[g][cur]], writes=[psr[4 + hf]])
                                    S.op("pe", mm(ps[6 + hf][:, 0:NCH], bt[hp * 64:(hp + 1) * 64, :], u_[hp * 64:(hp + 1) * 64, csl], True, True),
                                         reads=btr + u_r, writes=[psr[6 + hf]])
                                S.op("dve", lambda e, g=g, blk=blk, hsl=hsl, hp=hp: e.tensor_tensor(H[g][:].rearrange("p (r c) -> p r c", c=64), H[g][:].rearrange("p (r c) -> p r c", c=64),
                                                                                                cdec[hp][:, blk, hsl].unsqueeze(2).to_broadcast([128, R, 64]), ALU.mult),
                                     reads=[r_sm, r_H[g]], writes=[r_H[g]])
                                for hf in range(NHH):
                                    csl = slice(hf * NCH, (hf + 1) * NCH)
                                    S.op("dve", lambda e, g=g, csl=csl, hf=hf: e.tensor_tensor(H[g][:, csl], H[g][:, csl], ps[6 + hf][:, 0:NCH], ALU.add),
                                         reads=[psr[6 + hf], r_H[g]], writes=[r_H[g]])
                                S.op("act", lambda e, g=g, nxt=nxt: e.activation(out=Hb[g][nxt][:], in_=H[g][:], func=AF.Identity), reads=[r_H[g]], writes=[r_Hb[g][nxt]])
                                hbi[g] = nxt
                            for hf in range(NHH):
                                csl = slice(hf * NCH, (hf + 1) * NCH)
                                t1, t1res = t1r.next()
                                hs2 = slice(g * R + hf * HPH, g * R + (hf + 1) * HPH)
                                S.op("dve", lambda e, t1=t1, hf=hf, blk=blk, hs2=hs2: e.tensor_tensor(t1[:].rearrange("p (r c) -> p r c", c=64), ps[4 + hf][:, 0:NCH].rearrange("p (r c) -> p r c", c=64),
                                                                                                 eacs[:, blk, hs2].unsqueeze(2).to_broadcast([128, HPH, 64]), ALU.mult),
                                     reads=[psr[4 + hf], r_sm], writes=t1res)
                                for q0 in range(0, HPH, 4):
                                    nq = min(4, HPH - q0)
                                    S.op("pe", [mm(ps[2][:, i * 128:(i + 1) * 128], lh[:, hf * HPH + q0 + i, :], CF("MU"), True, True) for i in range(nq)],
                                         reads=lhres + [r_c], writes=[psr[2]])
                                    E_, Er = er.next()
                                    S.op("act", lambda e, E_=E_, nq=nq: e.activation(out=E_[:, 0:nq * 128], in_=ps[2][:, 0:nq * 128], func=AF.Exp), reads=[psr[2]], writes=Er)
                                    mt, mtres = mtr.next()
                                    S.op("dve", lambda e, mt=mt, E_=E_, cm_=cm_, nq=nq: e.tensor_tensor(mt[:, 0:nq, :], E_[:, 0:nq * 128].rearrange("p (r c) -> p r c", c=128),
                                                                                                  cm_[:].unsqueeze(1).to_broadcast([128, nq, 128]), ALU.mult),
                                         reads=Er + cmr, writes=mtres)
                                    S.op("pe", [mm(ps[3][:, (q0 + i) * 64:(q0 + i + 1) * 64], mt[:, i, :], xdt[:, hf * NCH + (q0 + i) * 64:hf * NCH + (q0 + i + 1) * 64], True, True) for i in range(nq)],
                                         reads=mtres + xdr, writes=[psr[3]])
                                S.op("dve", lambda e, t1=t1: e.tensor_tensor(t1[:], t1[:], ps[3][:, 0:NCH], ALU.add), reads=[psr[3]] + t1res, writes=t1res)
                                S.op("dve", lambda e, t1=t1, xD=xD, csl=csl: e.tensor_tensor(t1[:], t1[:], xD[:, csl], ALU.add), reads=xDres + t1res, writes=t1res)
                                pzb = ps[0][:].bitcast(BF16)
                                nbz = NCH // 128
                                S.op("pe", [lambda e, i=i, sz=sz, tsl=tsl, pzb=pzb, hf=hf, nbz=nbz: e.transpose(pzb[:, i * 128:(i + 1) * 128], sz[:, hf * nbz + i, tsl], CB("ident")) for i in range(nbz)],
                                     reads=szres + [r_c, psr[0]], writes=[psr[0]])
                                S.op("dve", lambda e, yz=yz, t1=t1, csl=csl, pzb=pzb: e.tensor_tensor(yz[:, csl], t1[:], pzb[:, 0:NCH], ALU.mult), reads=[psr[0]] + t1res, writes=yzres)
                            sq, sqres = sqr.next()
                            ss, ssres = ssr.next()
                            S.op("dve", lambda e, ss=ss: e.memset(ss[:], 0.0), writes=ssres)
                            S.op("act", lambda e, sq=sq, yz=yz, ss=ss: e.activation(out=sq[:], in_=yz[:], func=AF.Square, accum_out=ss[:, 0:1]), reads=yzres + ssres, writes=sqres + ssres)
                            S.op("dve", lambda e, ss=ss: e.tensor_scalar(ss[:, 1:2], ss[:, 0:1], 1.0 / GW, RMS_EPS, ALU.mult, ALU.add), reads=ssres, writes=ssres)
                            S.op("act", lambda e, ss=ss: e.activation(out=ss[:, 1:2], in_=ss[:, 1:2], func=AF.Sqrt), reads=ssres, writes=ssres)
                            S.op("dve", lambda e, ss=ss: e.reciprocal(ss[:, 1:2], ss[:, 1:2]), reads=ssres, writes=ssres)
                            yn, ynres = ynr.next()
                            S.op("dve", lambda e, yn=yn, yz=yz, ss=ss, g=g: e.scalar_tensor_tensor(out=yn[:], in0=yz[:], scalar=ss[:, 1:2], in1=nrw[:, g * GW:(g + 1) * GW], op0=ALU.mult, op1=ALU.mult),
                                 reads=yzres + ssres + [r_p], writes=ynres)
                            pyb = ps[2][:].bitcast(BF16)
                            S.op("pe", [lambda e, b_=b_, yn=yn, pyb=pyb: e.transpose(pyb[:, b_ * 128:(b_ + 1) * 128], yn[:, b_ * 128:(b_ + 1) * 128], CB("ident")) for b_ in range(GB)],
                                 reads=ynres + [r_c], writes=[psr[2]])
                            S.op("act", lambda e, ysg=ysg, pyb=pyb, tsl=tsl: e.activation(out=ysg[:, :, tsl], in_=pyb[:, 0:GW].rearrange("p (b c) -> p b c", c=128), func=AF.Identity),
                                 reads=[psr[2]], writes=ysres)
                        fr_ = front(0)
                        for tb in range(4):
                            nfr = front(tb + 1) if tb < 3 else None
                            back(tb, fr_)
                            fr_ = nfr
                        S.dma("act", lambda e, ysg=ysg, g=g, sc=sc: e.dma_start(out=ycat[YB_S * 128 + g * GW:YB_S * 128 + (g + 1) * GW, sc * 512:(sc + 1) * 512].rearrange("(j p) t -> p j t", p=128), in_=ysg[:]),
                              reads=ysres, writes=[r_ycb[YB_S + g * GB + b_] for b_ in range(GB)])
                S.barrier()
            if doX:
                emit_coll(YB_A)
            scale = 1.0 / np.sqrt(128.0)
            with contextlib.ExitStack() as st:
                qkv = Ring(nc, st, "qkv", 2, [128, 3, L], BF16)
                vtok = st.enter_context(nc.sbuf_tensor("vtok", [128, NBLKT, 128], BF16))
                r_vtok = Res()
                boff = st.enter_context(nc.sbuf_tensor("boff", [128, 8, NBLKT], F32))
                bdg = st.enter_context(nc.sbuf_tensor("bdg", [128, 8, 4, 4], F32))
                gq = st.enter_context(nc.sbuf_tensor("gq", [128, NBLKT], F32))
                r_btab = Res()
                ptr = Ring(nc, st, "pt", 4, [128, 512], BF16)
                rin = Ring(nc, st, "rinv", 4, [128, 512], F32)
                yst = Ring(nc, st, "yst", 8, [128, 512], BF16)
                hb0 = 2 * (2 * GB + 2)
                for hh in range(FHC):
                    qt, qr = qkv.next()
                    j0 = hb0 + 3 * hh
                    S.dma("sp", lambda e, qt=qt, j0=j0: e.dma_start(out=qt[:], in_=projT[j0 * 128:(j0 + 3) * 128, :].rearrange("(j p) t -> p j t", p=128)),
                          reads=[r_proj], writes=qr)
                    for g4 in range(NBLKT // 8):
                        pv = ps[2][:].bitcast(BF16)
                        S.op("pe", [lambda e, g4=g4, i=i, pv=pv, qt=qt: e.transpose(pv[:, i * 128:(i + 1) * 128], qt[:, 2, (g4 * 8 + i) * 128:(g4 * 8 + i + 1) * 128], CB("ident"))
                                    for i in range(8)], reads=qr + [r_c], writes=[psr[2]])
                        S.op("dve", lambda e, g4=g4, pv=pv: e.tensor_copy(vtok[:, g4 * 8:(g4 + 1) * 8, :].rearrange("p b d -> p (b d)"), pv),
                             reads=[psr[2]], writes=[r_vtok])
                    Fq4 = Fq[:, :, hh].rearrange("p (c j) -> p c j", j=4)
                    Ft4 = Ftok[:, :, hh].rearrange("p (c j) -> p c j", j=4)
                    S.op("dve", lambda e, hh=hh, Fq4=Fq4: e.tensor_tensor(boff[:], Fq4[:, :, 0:1].to_broadcast([128, 8, NBLKT]),
                                                                      Ftok[:, :, hh].unsqueeze(1).to_broadcast([128, 8, NBLKT]), ALU.subtract),
                         reads=[r_sm], writes=[r_btab])
                    S.op("dve", lambda e, Fq4=Fq4, Ft4=Ft4: e.tensor_tensor(bdg[:], Fq4.unsqueeze(2).to_broadcast([128, 8, 4, 4]),
                                                                        Ft4.unsqueeze(3).to_broadcast([128, 8, 4, 4]), ALU.subtract),
                         reads=[r_sm], writes=[r_btab])
                    S.op("dve", lambda e, Fq4=Fq4: e.tensor_tensor(gq[:].rearrange("p (c j) -> p c j", j=4), Fq4, Fq4[:, :, 0:1].to_broadcast([128, 8, 4]), ALU.subtract),
                         reads=[r_sm], writes=[r_btab])
                    S.op("act", lambda e: e.activation(out=gq[:], in_=gq[:], func=AF.Exp), reads=[r_btab], writes=[r_btab])
                    pairs = [(c, kb) for c in range(8) for kb in range(4 * c + 4)]
                    sbanks = [0, 1, 5]
                    LOOK = 2

                    def emit_S(n):
                        c, kb = pairs[n]
                        sb = sbanks[n % 3]
                        S.op("pe", mm(ps[sb][:], qt[:, 1, kb * 128:(kb + 1) * 128], qt[:, 0, c * 512:(c + 1) * 512], True, True),
                             reads=qr, writes=[psr[sb]])

                    for n in range(min(LOOK, len(pairs))):
                        emit_S(n)
                    for n, (c, kb) in enumerate(pairs):
                        if n + LOOK < len(pairs):
                            emit_S(n + LOOK)
                        nkb = 4 * c + 4
                        sb = sbanks[n % 3]
                        pt, pr = ptr.next()
                        if kb < 4 * c:
                            S.op("act", lambda e, pt=pt, sb=sb, c=c, kb=kb: e.activation(out=pt[:], in_=ps[sb][:], func=AF.Exp,
                                                                                  bias=boff[:, c, kb:kb + 1], scale=float(scale)), reads=[psr[sb], r_btab], writes=pr)
                            S.op("pe", [mm(ps[3][:], vtok[:, kb, :], pt[:], kb == 0, kb == 4 * c - 1),
                                        mm(ps[4][:], CB("ONES"), pt[:], kb == 0, kb == 4 * c - 1)],
                                 reads=pr + [r_vtok, r_c], writes=[psr[3], psr[4]])
                        else:
                            jd = kb - 4 * c
                            for j in range(jd, 4):
                                S.op("act", lambda e, pt=pt, sb=sb, j=j, c=c, jd=jd: e.activation(
                                    out=pt[:, j * 128:(j + 1) * 128], in_=ps[sb][:, j * 128:(j + 1) * 128], func=AF.Exp,
                                    bias=bdg[:, c, jd, j:j + 1], scale=float(scale)), reads=[psr[sb], r_btab], writes=pr)
                            S.op("dve", lambda e, pt=pt, jd=jd: e.tensor_tensor(pt[:, jd * 128:(jd + 1) * 128], pt[:, jd * 128:(jd + 1) * 128], CB("CM"), ALU.mult),
                                 reads=pr + [r_c], writes=pr)
                            S.op("pe", [mm(ps[6][:, jd * 128:512], vtok[:, kb, :], pt[:, jd * 128:512], jd == 0, jd == 3),
                                        mm(ps[7][:, jd * 128:512], CB("ONES"), pt[:, jd * 128:512], jd == 0, jd == 3)],
                                 reads=pr + [r_vtok, r_c], writes=[psr[6], psr[7]])
                        if kb == nkb - 1:
                            ri, rr = rin.next()
                            ys, yr = yst.next()
                            if c == 0:
                                S.op("dve", lambda e, ri=ri: e.reciprocal(ri[:], ps[7][:]), reads=[psr[7]], writes=rr)
                                S.op("dve", lambda e, ri=ri, ys=ys: e.tensor_tensor(ys[:], ps[6][:], ri[:], ALU.mult), reads=[psr[6]] + rr, writes=yr)
                            else:
                                to, tor = rin.next()
                                gbc = gq[:, 4 * c:4 * c + 4].unsqueeze(2).to_broadcast([128, 4, 128])
                                S.op("dve", lambda e, ri=ri, gbc=gbc: e.tensor_tensor(ri[:].rearrange("p (j q) -> p j q", q=128), ps[4][:].rearrange("p (j q) -> p j q", q=128), gbc, ALU.mult),
                                     reads=[psr[4], r_btab], writes=rr)
                                S.op("dve", lambda e, ri=ri: e.tensor_tensor(ri[:], ri[:], ps[7][:], ALU.add), reads=[psr[7]] + rr, writes=rr)
                                S.op("dve", lambda e, ri=ri: e.reciprocal(ri[:], ri[:]), reads=rr, writes=rr)
                                S.op("dve", lambda e, to=to, gbc=gbc: e.tensor_tensor(to[:].rearrange("p (j q) -> p j q", q=128), ps[3][:].rearrange("p (j q) -> p j q", q=128), gbc, ALU.mult),
                                     reads=[psr[3], r_btab], writes=tor)
                                S.op("dve", lambda e, to=to: e.tensor_tensor(to[:], to[:], ps[6][:], ALU.add), reads=[psr[6]] + tor, writes=tor)
                                S.op("dve", lambda e, to=to, ri=ri, ys=ys: e.tensor_tensor(ys[:], to[:], ri[:], ALU.mult), reads=tor + rr, writes=yr)
                            yb = YB_A + hh
                            S.dma("act", lambda e, ys=ys, yb=yb, c=c: e.dma_start(out=ycat[yb * 128:(yb + 1) * 128, c * 512:(c + 1) * 512], in_=ys[:]),
                                  reads=yr, writes=[r_ycb[yb]])
                    if doX:
                        emit_coll(YB_A + hh + 1)
                S.barrier()

            stA.close()

        if doX:
            emit_coll(NCHK)
            if yall_dbg is not None:
                for r_ in range(4):
                    S.dma("sp", lambda e, r_=r_: e.dma_start(out=yall_dbg[r_ * YROWS:(r_ + 1) * YROWS, :], in_=yall[r_ * YROWS:(r_ + 1) * YROWS, :]), reads=[r_yall], writes=[Res()])
            S.barrier(with_cc=True)

        if doB:
            KY = 4 * (2 * GB + FHC)
            NBR = 2 * GB + FHC
            kyblocks = [r * YCB + YB_S + i for r in range(4) for i in range(NBR)]
            ssd_pos = [r * NBR + i for r in range(4) for i in range(2 * GB)]
            att_pos = [r * NBR + 2 * GB + i for r in range(4) for i in range(FHC)]
            with contextlib.ExitStack() as st:
                yw = ywin.rearrange("(j p) t -> p j t", p=128)
                prmB = st.enter_context(nc.sbuf_tensor("prmB", [128, 6, KC], F32))
                fcwt = st.enter_context(nc.sbuf_tensor("fcwt", [128, 2 * NFB, 3], F32))
                fcbt = st.enter_context(nc.sbuf_tensor("fcbt", [128, 2 * NFB], F32))
                hmk = st.enter_context(nc.sbuf_tensor("hmk", [128, 1], F32))
                usave = st.enter_context(nc.sbuf_tensor("usave", [128, 2 * NFB, 2], F32))
                r_pb, r_us = Res(), Res()
                S.dma("sp", lambda e: e.dma_start(out=prmB[:, 0:2, :], in_=gbias[:, :, :]), writes=[r_pb])
                S.dma("sp", lambda e: e.dma_start(out=prmB[:, 2:6, :], in_=lng[:, :, :]), writes=[r_pb])
                S.dma("sp", lambda e: e.dma_start(out=fcwt[:], in_=fcw[:, :, :]), writes=[r_pb])
                S.dma("sp", lambda e: e.dma_start(out=fcbt[:], in_=fcb[:, :]), writes=[r_pb])
                S.dma("sp", lambda e: e.dma_start(out=hmk[:], in_=hmask[:, :]), writes=[r_pb])
                wring = Ring(nc, st, "wB", 4, [128, 32, 128], BF16)
                tmpr = Ring(nc, st, "tB", 6, [128, 514], F32)
                gring = Ring(nc, st, "gB", 4, [128, 2, 514], BF16)
                mean = st.enter_context(nc.sbuf_tensor("mean", [128, 514], F32))
                rstd = st.enter_context(nc.sbuf_tensor("rstd", [128, 514], F32))
                r_st = Res()
                yT = st.enter_context(nc.sbuf_tensor("yT", [128, max(KY, NFB), 514], BF16))
                mrg = st.enter_context(nc.sbuf_tensor("mrg", [128, KC, 514], BF16))
                r_yT, r_mrg = Res(), Res()
                r_hsc, r_h1n, r_out = Res(), Res(), Res()

                def load_piece(src, k0, kn):
                    wt, wr = wring.next()
                    S.dma("pool", lambda e, wt=wt: e.dma_start(out=wt[:, 0:kn, :], in_=src[:, k0:k0 + kn, :]), writes=wr)
                    return wt, wr

                def gemm(segs, pieces, act_fn, act_res, bank_m, bank_h):
                    tot = sum(p[2] for p in pieces)
                    done = 0
                    for (src, k0, kn, kidx) in pieces:
                        wt, wr = load_piece(src, k0, kn)
                        fl = []
                        for i in range(kn):
                            for (c0, c1, isold) in segs:
                                bank = bank_h if isold else bank_m
                                o = ps[bank][:, 0:c1 - c0]
                                fl.append(mm(o, wt[:, i, :], act_fn(kidx[i], c0, c1), done + i == 0, done + i == tot - 1))
                        S.op("pe", fl, reads=wr + act_res, writes=[psr[bank_m]] + ([psr[bank_h]] if len(segs) > 1 else []))
                        done += kn

                def ln_stats_acc(segs, src_tile, src_res, j):
                    sqt, sqres = tmpr.next()
                    S.op("act", lambda e: e.activation(out=sqt[:, 0:segs[-1][1]], in_=src_tile[:, 0:segs[-1][1]], func=AF.Square), reads=src_res, writes=sqres)
                    fl = []
                    for (c0, c1, ish) in segs:
                        if not ish:
                            fl.append(mm(ps[4][:, 0:c1 - c0], CF("ONES"), src_tile[:, c0:c1], j == 0, j == KC - 1))
                            fl.append(mm(ps[5][:, 0:c1 - c0], CF("ONES"), sqt[:, c0:c1], j == 0, j == KC - 1))
                        else:
                            fl.append(mm(ps[6][:, 0:2], CF("ONES"), src_tile[:, c0:c1], j == 0, j == KC - 1))
                            fl.append(mm(ps[6][:, 2:4], CF("ONES"), sqt[:, c0:c1], j == 0, j == KC - 1))
                    S.op("pe", fl, reads=src_res + sqres + [r_c], writes=[psr[4], psr[5], psr[6]])

                def ln_finish(segs):
                    W = segs[-1][1]
                    for (c0, c1, ish) in segs:
                        sm = ps[6][:, 0:2] if ish else ps[4][:, 0:c1 - c0]
                        sq_ = ps[6][:, 2:4] if ish else ps[5][:, 0:c1 - c0]
                        S.op("dve", lambda e, c0=c0, c1=c1, sm=sm: e.tensor_scalar_mul(mean[:, c0:c1], sm, 1.0 / D), reads=[psr[4], psr[6]], writes=[r_st])
                        S.op("dve", lambda e, c0=c0, c1=c1, sq_=sq_: e.tensor_scalar_mul(rstd[:, c0:c1], sq_, 1.0 / D), reads=[psr[5], psr[6]], writes=[r_st])
                    t_, tr = tmpr.next()
                    S.op("dve", lambda e: e.tensor_tensor(t_[:, 0:W], mean[:, 0:W], mean[:, 0:W], ALU.mult), reads=[r_st], writes=tr)
                    S.op("dve", lambda e: e.tensor_tensor(rstd[:, 0:W], rstd[:, 0:W], t_[:, 0:W], ALU.subtract), reads=[r_st] + tr, writes=[r_st])
                    S.op("dve", lambda e: e.tensor_scalar_add(rstd[:, 0:W], rstd[:, 0:W], LN_EPS), reads=[r_st], writes=[r_st])
                    S.op("act", lambda e: e.activation(out=rstd[:, 0:W], in_=rstd[:, 0:W], func=AF.Sqrt), reads=[r_st], writes=[r_st])
                    S.op("dve", lambda e: e.reciprocal(rstd[:, 0:W], rstd[:, 0:W]), reads=[r_st], writes=[r_st])

                for pi in range(2):
                    segs = [(0, 512, False), (512, 514, True)] if pi == 0 else [(0, 512, False)]
                    W = segs[-1][1]
                    col0 = 2 + pi * 512
                    if doX:
                        ya4 = yall.rearrange("(k r p) t -> k r p t", r=4, p=128)
                        for r_ in range(4):
                            S.dma("sp", lambda e, r_=r_, pi=pi: e.dma_start(
                                out=yT[:, r_ * NBR:(r_ + 1) * NBR, 0:512],
                                in_=ya4[YB_S:YCB, r_, :, bass.ds(dv(e, 2 * pi, "sp"), 512)].rearrange("j p t -> p j t")),
                                reads=[r_yall], writes=[r_yT])
                            if pi == 0:
                                S.dma("sp", lambda e, r_=r_: e.dma_start(
                                    out=yT[:, r_ * NBR:(r_ + 1) * NBR, 512:514],
                                    in_=ya4[YB_S:YCB, r_, :, bass.ds(dv(e, 1, "sp"), 2)].rearrange("j p t -> p j t")),
                                    reads=[r_yall], writes=[r_yT])
                    else:
                        for r_ in range(4):
                            src0 = r_ * YCB + YB_S
                            S.dma("sp", lambda e, r_=r_, src0=src0, pi=pi: e.dma_start(
                                out=yT[:, r_ * NBR:(r_ + 1) * NBR, 0:512], in_=yw[:, src0:src0 + NBR, 2 + pi * 512:2 + (pi + 1) * 512]),
                                reads=[r_yw], writes=[r_yT])
                            if pi == 0:
                                S.dma("sp", lambda e, r_=r_, src0=src0: e.dma_start(
                                    out=yT[:, r_ * NBR:(r_ + 1) * NBR, 512:514], in_=yw[:, src0:src0 + NBR, 0:2]),
                                    reads=[r_yw], writes=[r_yT])
                    nss = len(ssd_pos)
                    for j in range(KC):
                        gts = []
                        for gi in range(2):
                            gt, gr = gring.next()
                            rb = (j // KG) * YCB + YB_G + gi * KG + (j % KG)
                            rb0 = min(rb, 4 * YCB - 2)
                            gsel = rb - rb0
                            S.dma("sp", lambda e, gt=gt, rb0=rb0, pi=pi: e.dma_start(out=gt[:, :, 0:512], in_=yw[:, rb0:rb0 + 2, 2 + pi * 512:2 + (pi + 1) * 512]),
                                  reads=[r_yw], writes=gr)
                            if pi == 0:
                                S.dma("sp", lambda e, gt=gt, rb0=rb0: e.dma_start(out=gt[:, :, 512:514], in_=yw[:, rb0:rb0 + 2, 0:2]),
                                      reads=[r_yw], writes=gr)
                            sg, sgr = tmpr.next()
                            S.op("act", lambda e, sg=sg, gt=gt, gi=gi, j=j, W=W, gsel=gsel: e.activation(out=sg[:, 0:W], in_=gt[:, gsel, 0:W], func=AF.Sigmoid, bias=prmB[:, gi, j:j + 1]),
                                 reads=gr + [r_pb], writes=sgr)
                            gts.append((sg, sgr))
                        act_y = lambda k, c0, c1: yT[:, k, c0:c1]
                        bA, bB, hA, hB = (0, 1, 2, 3) if j % 2 == 0 else (4, 5, 6, 7)
                        gemm(segs, [(wps[j], 0, min(32, nss), ssd_pos[0:min(32, nss)])] +
                             ([(wps[j], 32, nss - 32, ssd_pos[32:nss])] if nss > 32 else []), act_y, [r_yT], bA, hA)
                        gemm(segs, [(wpa[j], 0, KC, att_pos)], act_y, [r_yT], bB, hB)
                        m1, m1r = tmpr.next()
                        for (c0, c1, ish) in segs:
                            pa = ps[hA][:, 0:2] if ish else ps[bA][:]
                            pb = ps[hB][:, 0:2] if ish else ps[bB][:]
                            S.op("dve", lambda e, c0=c0, c1=c1, pa=pa, a=gts[0][0]: e.tensor_tensor(a[:, c0:c1], a[:, c0:c1], pa, ALU.mult), reads=[psr[bA], psr[hA]] + gts[0][1], writes=gts[0][1])
                            S.op("dve", lambda e, c0=c0, c1=c1, pb=pb, b=gts[1][0]: e.tensor_tensor(b[:, c0:c1], b[:, c0:c1], pb, ALU.mult), reads=[psr[bB], psr[hB]] + gts[1][1], writes=gts[1][1])
                        S.op("dve", lambda e, j=j, W=W, a=gts[0][0], b=gts[1][0]: e.tensor_tensor(mrg[:, j, 0:W], a[:, 0:W], b[:, 0:W], ALU.add), reads=gts[0][1] + gts[1][1], writes=[r_mrg])
                    for j in range(KC):
                        b2m, b2h = (0, 2) if j % 2 == 0 else (1, 3)
                        gemm(segs, [(wo[j], 0, KC, list(range(KC)))], lambda k, c0, c1: mrg[:, k, c0:c1], [r_mrg], b2m, b2h)
                        xt_, xr = tmpr.next()
                        S.dma("sp", lambda e, xt_=xt_, j=j, pi=pi: e.dma_start(out=xt_[:, 0:512], in_=xTs[j * 128:(j + 1) * 128, 2 + pi * 512:2 + (pi + 1) * 512]), writes=xr)
                        if pi == 0:
                            S.dma("sp", lambda e, xt_=xt_, j=j: e.dma_start(out=xt_[:, 512:514], in_=xTs[j * 128:(j + 1) * 128, 0:2]), writes=xr)
                        for (c0, c1, ish) in segs:
                            pa = ps[b2h][:, 0:2] if ish else ps[b2m][:]
                            S.op("dve", lambda e, xt_=xt_, c0=c0, c1=c1, pa=pa: e.scalar_tensor_tensor(out=xt_[:, c0:c1], in0=xt_[:, c0:c1], scalar=float(ALPHA), in1=pa, op0=ALU.mult, op1=ALU.add),
                                 reads=[psr[b2m], psr[b2h]] + xr, writes=xr)
                        ln_stats_acc(segs, xt_, xr, j)
                        S.dma("act", lambda e, xt_=xt_, j=j, W=W: e.dma_start(out=hsc[j * 128:(j + 1) * 128, 0:W], in_=xt_[:, 0:W]), reads=xr, writes=[r_hsc])
                    ln_finish(segs)
                    for j in range(KC):
                        ht, hr = tmpr.next()
                        S.dma("sp", lambda e, ht=ht, j=j, W=W: e.dma_start(out=ht[:, 0:W], in_=hsc[j * 128:(j + 1) * 128, 0:W]), reads=[r_hsc], writes=hr)
                        S.op("dve", lambda e, ht=ht, W=W: e.tensor_tensor(ht[:, 0:W], ht[:, 0:W], mean[:, 0:W], ALU.subtract), reads=hr + [r_st], writes=hr)
                        S.op("dve", lambda e, ht=ht, W=W: e.tensor_tensor(ht[:, 0:W], ht[:, 0:W], rstd[:, 0:W], ALU.mult), reads=hr + [r_st], writes=hr)
                        S.op("act", lambda e, ht=ht, W=W, j=j: e.activation(out=ht[:, 0:W], in_=ht[:, 0:W], func=AF.Identity, scale=prmB[:, 2, j:j + 1], bias=prmB[:, 3, j:j + 1]),
                             reads=hr + [r_pb], writes=hr)
                        S.op("dve", lambda e, ht=ht, W=W, j=j: e.tensor_copy(mrg[:, j, 0:W], ht[:, 0:W]), reads=hr, writes=[r_mrg])
                        S.dma("act", lambda e, ht=ht, j=j: e.dma_start(out=h1n[j * 128:(j + 1) * 128, 0:512], in_=ht[:, 0:512]), reads=hr, writes=[r_h1n])
                    for jf in range(NFB):
                        cvs = []
                        for vi in range(2):
                            jj = vi * NFB + jf
                            gemm(segs, [(wup[jj], 0, KC, list(range(KC)))], lambda k, c0, c1: mrg[:, k, c0:c1], [r_mrg], vi, 2 + vi)
                            ue, uer = tmpr.next()
                            S.op("act", lambda e, ue=ue, vi=vi: e.activation(out=ue[:, 2:514], in_=ps[vi][:], func=AF.Identity), reads=[psr[vi]], writes=uer)
                            if pi == 0:
                                S.op("dve", lambda e, ue=ue, vi=vi: e.tensor_scalar_mul(ue[:, 0:2], ps[2 + vi][:, 0:2], hmk[:, 0:1]), reads=[psr[2 + vi], r_pb], writes=uer)
                                S.op("dve", lambda e, ue=ue, jj=jj: e.tensor_copy(usave[:, jj, :], ue[:, 512:514]), reads=uer, writes=[r_us])
                            else:
                                S.op("dve", lambda e, ue=ue, jj=jj: e.tensor_copy(ue[:, 0:2], usave[:, jj, :]), reads=[r_us], writes=uer)
                            cv, cvr = tmpr.next()
                            S.op("act", lambda e, cv=cv, ue=ue, jj=jj: e.activation(out=cv[:, 0:512], in_=ue[:, 2:514], func=AF.Identity, scale=fcwt[:, jj, 2:3], bias=fcbt[:, jj:jj + 1]),
                                 reads=uer + [r_pb], writes=cvr)
                            for k in (1, 0):
                                S.op("dve", lambda e, cv=cv, ue=ue, jj=jj, k=k: e.scalar_tensor_tensor(out=cv[:, 0:512], in0=ue[:, k:k + 512], scalar=fcwt[:, jj, k:k + 1], in1=cv[:, 0:512], op0=ALU.mult, op1=ALU.add),
                                     reads=uer + cvr + [r_pb], writes=cvr)
                            cvs.append((cv, cvr))
                        S.op("act", lambda e, cv=cvs[1][0]: e.activation(out=cv[:, 0:512], in_=cv[:, 0:512], func=AF.Silu), reads=cvs[1][1], writes=cvs[1][1])
                        S.op("dve", lambda e, jf=jf, a=cvs[0][0], b=cvs[1][0]: e.tensor_tensor(yT[:, jf, 0:512], a[:, 0:512], b[:, 0:512], ALU.mult), reads=cvs[0][1] + cvs[1][1], writes=[r_yT])
                    segm = [(0, 512, False)]
                    for j in range(KC):
                        pcs = []
                        k0 = 0
                        while k0 < NFB:
                            kn = min(32, NFB - k0)
                            pcs.append((wdn[j], k0, kn, list(range(k0, k0 + kn))))
                            k0 += kn
                        b4 = j % 2
                        gemm(segm, pcs, lambda k, c0, c1: yT[:, k, c0:c1], [r_yT], b4, 2)
                        ht, hr = tmpr.next()
                        S.dma("sp", lambda e, ht=ht, j=j: e.dma_start(out=ht[:, 0:512], in_=h1n[j * 128:(j + 1) * 128, 0:512]), reads=[r_h1n], writes=hr)
                        S.op("dve", lambda e, ht=ht, b4=b4: e.scalar_tensor_tensor(out=ht[:, 0:512], in0=ht[:, 0:512], scalar=float(ALPHA), in1=ps[b4][:], op0=ALU.mult, op1=ALU.add),
                             reads=[psr[b4]] + hr, writes=hr)
                        ln_stats_acc(segm, ht, hr, j)
                        S.dma("act", lambda e, ht=ht, j=j: e.dma_start(out=hsc[j * 128:(j + 1) * 128, 0:512], in_=ht[:, 0:512]), reads=hr, writes=[r_hsc])
                    ln_finish(segm)
                    for j in range(KC):
                        ht, hr = tmpr.next()
                        S.dma("sp", lambda e, ht=ht, j=j: e.dma_start(out=ht[:, 0:512], in_=hsc[j * 128:(j + 1) * 128, 0:512]), reads=[r_hsc], writes=hr)
                        S.op("dve", lambda e, ht=ht: e.tensor_tensor(ht[:, 0:512], ht[:, 0:512], mean[:, 0:512], ALU.subtract), reads=hr + [r_st], writes=hr)
                        S.op("dve", lambda e, ht=ht: e.tensor_tensor(ht[:, 0:512], ht[:, 0:512], rstd[:, 0:512], ALU.mult), reads=hr + [r_st], writes=hr)
                        S.op("act", lambda e, ht=ht, j=j: e.activation(out=ht[:, 0:512], in_=ht[:, 0:512], func=AF.Identity, scale=prmB[:, 4, j:j + 1], bias=prmB[:, 5, j:j + 1]),
                             reads=hr + [r_pb], writes=hr)
                        S.dma("act", lambda e, ht=ht, j=j, pi=pi: e.dma_start(out=outT[j * 128:(j + 1) * 128, pi * 512:(pi + 1) * 512], in_=ht[:, 0:512]), reads=hr, writes=[r_out])
                S.barrier()
        S.final_wait()
        S.emit()
    return nc, in_names


def blockify(W):
    K, N = W.shape
    return np.ascontiguousarray(W.reshape(K // 128, 128, N // 128, 128).transpose(2, 1, 0, 3))


def pcols(v, nb):
    return np.ascontiguousarray(np.asarray(v, dtype=np.float32).reshape(nb, 128).T)


def bc(v):
    return np.ascontiguousarray(np.broadcast_to(np.asarray(v, dtype=np.float32)[None, :], (128, len(v))))


def prep_inputs(D, inp):
    dm = Dims(D)
    KC, GB, GW, R, FHC, KG = dm.KC, dm.GB, dm.GW, dm.R, dm.FHC, dm.KG
    x = np.asarray(inp["x"], dtype=np.float32)
    w_in = np.asarray(inp["w_in"], dtype=np.float32)[0]
    cw = np.asarray(inp["ssd_conv_w"], dtype=np.float32)[0]
    cbv = np.asarray(inp["ssd_conv_b"], dtype=np.float32)[0]
    shared = {
        "consts": CONST_ARR,
        "gbias": np.ascontiguousarray(np.stack([pcols(inp["gate_bias"][0][0], KC), pcols(inp["gate_bias"][0][1], KC)], axis=1)),
        "wps": blockify(np.asarray(inp["w_proj_ssd"], dtype=np.float32)[0]),
        "wpa": blockify(np.asarray(inp["w_proj_att"], dtype=np.float32)[0]),
        "wo": blockify(np.asarray(inp["w_out"], dtype=np.float32)[0]),
        "wup": blockify(np.asarray(inp["w_up"], dtype=np.float32)[0]),
        "wdn": blockify(np.asarray(inp["w_down"], dtype=np.float32)[0]),
        "lng": np.ascontiguousarray(np.stack([pcols(inp["ln1_g"][0], KC), pcols(inp["ln1_b"][0], KC), pcols(inp["ln2_g"][0], KC), pcols(inp["ln2_b"][0], KC)], axis=1)),
        "fcw": np.ascontiguousarray(np.asarray(inp["ffn_conv_w"], dtype=np.float32)[0].T.reshape(2 * dm.NFB, 128, 3).transpose(1, 0, 2)),
        "fcb": pcols(inp["ffn_conv_b"][0], 2 * dm.NFB),
    }
    xTs = [np.ascontiguousarray(x[b].T) for b in range(NB)]
    per_core = []
    for c in range(8):
        b, hg = c // 4, c % 4
        cols, ccols = [], []
        for gi in range(2):
            g = 2 * hg + gi
            xc_ = list(range(dm.oX + g * GW, dm.oX + (g + 1) * GW))
            bc_ = list(range(dm.oB + g * 128, dm.oB + (g + 1) * 128))
            cc_ = list(range(dm.oC + g * 128, dm.oC + (g + 1) * 128))
            cols += xc_ + bc_ + cc_ + list(range(dm.oZ + g * GW, dm.oZ + (g + 1) * GW))
            ccols += [i - dm.DI for i in xc_ + bc_ + cc_]
        for hh in range(FHC):
            h = hg * FHC + hh
            for o in (dm.oQ, dm.oK, dm.oV):
                cols += list(range(o + h * 128, o + (h + 1) * 128))
        for o in (dm.oG1, dm.oG2):
            cols += list(range(o + hg * KG * 128, o + (hg + 1) * KG * 128))
        heads = [2 * hg * R + i for i in range(2 * R)]
        scols = [dm.oDT + h for h in heads] + [dm.oF + hg * FHC + hh for hh in range(FHC)]
        chans = []
        for gi in range(2):
            g = 2 * hg + gi
            chans += list(range(g * GW, (g + 1) * GW))
        t0 = hg * 1024
        xs = np.zeros((D, 1026), dtype=np.float32)
        xs[:, 2:] = xTs[b][:, t0:t0 + 1024]
        if t0 > 0:
            xs[:, 0:2] = xTs[b][:, t0 - 2:t0]
        m = dict(shared)
        m.update({
            "xT": xTs[b],
            "wA": blockify(w_in[:, cols]),
            "wS": np.ascontiguousarray(w_in[:, scols].reshape(KC, 128, len(scols)).transpose(1, 0, 2)),
            "convw": np.ascontiguousarray(cw[:, ccols].T.reshape(2 * (GB + 2), 128, 4).transpose(1, 0, 2)),
            "convb": pcols(cbv[ccols], 2 * (GB + 2)),
            "dtb": bc(np.asarray(inp["ssd_dt_bias"], dtype=np.float32)[0][heads]),
            "alog": bc(np.asarray(inp["ssd_a_log"], dtype=np.float32)[0][heads]),
            "dskip": bc(np.repeat(np.asarray(inp["ssd_d"], dtype=np.float32)[0][heads], 64)),
            "normw": bc(np.asarray(inp["ssd_norm_w"], dtype=np.float32)[0][chans]),
            "fbias": bc(np.asarray(inp["fox_f_bias"], dtype=np.float32)[0][hg * FHC:(hg + 1) * FHC]),
            "xTs": xs,
            "t0": np.array([[t0, max(t0 - 2, 0), t0 + 512]], dtype=np.int32),
            "hmask": np.full((128, 1), 0.0 if hg == 0 else 1.0, dtype=np.float32),
        })
        per_core.append(m)
    return per_core


_NC_CACHE = {}


def run(D, inp, stages=("A", "X", "B"), extra=None, debug=False):
    key = (D, tuple(stages), debug)
    if key not in _NC_CACHE:
        _NC_CACHE[key] = build_program(D, stages, debug)
    nc, decl = _NC_CACHE[key]
    per_core = prep_inputs(D, inp)
    maps = []
    for c in range(8):
        m = per_core[c]
        if extra is not None:
            m.update(extra[c])
        maps.append({k: v for k, v in m.items() if k in decl})
    res = run_bass_kernel_spmd(nc, maps, core_ids=list(range(8)))
    return res.results


def kernel(**inputs):
    D = 4096
    res = run(D, inputs)
    out = np.zeros((NB, L, D), dtype=np.float32)
    for c in range(8):
        b, tq = c // 4, c % 4
        out[b, tq * 1024:(tq + 1) * 1024, :] = np.asarray(res[c]["outT"]).T
    return out
```
